# Optimizing a Trainium2 kernel written in Bass

```python
import math
import jax
import jax.numpy as jnp
from jax import lax
import numpy as np

D_MODEL = 1024
BATCH = 8
SEQ = 2048
DEPTH = 2

GRID_W = 64
CTX_LEN = 256
HEAD_DIM = 64
MIX_WIDTH = 1024
N_MOD = 9
FFN_HIDDEN = 2816
NORM_EPS = 1e-6
FOURIER_GROUPS = 4
FOURIER_WIDTH = FOURIER_GROUPS * HEAD_DIM
GLA_HEADS = 6
GLA_DK = 64
GLA_DV = 128
GLA_GATE_RANK = 16
GLA_TAU = 16.0
GLA_CHUNK = 64
CONV_GROUPS = 4
CONV_WIDTH = CONV_GROUPS * HEAD_DIM
CONV_TAPS = 3
DIFF_HEADS = 6
DIFF_DV = 2 * HEAD_DIM
Q_BLOCK = 128
ROPE_THETA = 10000.0
ROPE_AXIS_DIM = HEAD_DIM // 2
N_EVEN = (DEPTH + 1) // 2
N_ODD = DEPTH // 2
EVEN_SPLITS = (FOURIER_WIDTH, GLA_HEADS * GLA_DK, GLA_HEADS * GLA_DK, GLA_HEADS * GLA_DV, GLA_HEADS * GLA_DV, GLA_GATE_RANK, GLA_GATE_RANK)
EVEN_IN = sum(EVEN_SPLITS)
ODD_SPLITS = (CONV_WIDTH, CONV_WIDTH, CONV_WIDTH, DIFF_HEADS * 2 * HEAD_DIM, DIFF_HEADS * 2 * HEAD_DIM, DIFF_HEADS * DIFF_DV)
ODD_IN = sum(ODD_SPLITS)

kernel_name = 'hybrid_fnet_gla_shortconv_diffattn_prefix_dit'


def _split(z, sizes):
    idx = [int(i) for i in np.cumsum(sizes)[:-1]]
    return jnp.split(z, idx, axis=-1)


def _rmsnorm(x, g):
    xf = x.astype(jnp.float32)
    y = xf * lax.rsqrt(jnp.mean(xf * xf, axis=-1, keepdims=True) + NORM_EPS)
    return (y * g.astype(jnp.float32)).astype(x.dtype)


def _modulation(cond, w, b):
    m = jax.nn.silu(cond) @ w + b
    return jnp.split(m[..., None, :], N_MOD, axis=-1)


def _prenorm(h, gain, shift, scale):
    return _rmsnorm(h, gain) * (1.0 + scale) + shift


def _swiglu(h, w_in, w_out):
    g, u = jnp.split(h @ w_in, 2, axis=-1)
    return (jax.nn.silu(g) * u) @ w_out


def _axial_rope_tables(rows):
    row = jnp.repeat(jnp.arange(rows), GRID_W).astype(jnp.float32)
    col = jnp.tile(jnp.arange(GRID_W), rows).astype(jnp.float32)
    n = ROPE_AXIS_DIM // 2
    inv = ROPE_THETA ** (-jnp.arange(n, dtype=jnp.float32) / n)
    ang = jnp.concatenate([row[:, None] * inv, col[:, None] * inv], axis=-1)
    return jnp.cos(ang), jnp.sin(ang)


def _axial_rope(x, cos, sin):
    n = ROPE_AXIS_DIM // 2
    def rot(part, cs, sn):
        p1, p2 = jnp.split(part, 2, axis=-1)
        return jnp.concatenate([p1 * cs - p2 * sn, p1 * sn + p2 * cs], axis=-1)
    bc = lambda t: t[:, None, None, :]
    xr = rot(x[..., :ROPE_AXIS_DIM], bc(cos[:, :n]), bc(sin[:, :n]))
    xc = rot(x[..., ROPE_AXIS_DIM:], bc(cos[:, n:]), bc(sin[:, n:]))
    return jnp.concatenate([xr, xc], axis=-1).astype(x.dtype)


def _fourier_mix(z):
    bn, t, _ = z.shape
    zf = z.astype(jnp.float32).reshape(bn, t, FOURIER_GROUPS, HEAD_DIM)
    y = jnp.fft.fft2(zf, axes=(1, 3), norm='ortho').real
    return y.reshape(bn, t, FOURIER_WIDTH).astype(z.dtype)


def _gla_chunked(q, k, v, logg, s0):
    bn, h, t, dk = q.shape
    dv = v.shape[-1]
    n = t // GLA_CHUNK
    def chunks(a):
        return jnp.moveaxis(a.reshape(bn, h, n, GLA_CHUNK, a.shape[-1]), 2, 0)
    b = jnp.cumsum(chunks(logg), axis=-2)
    mask = jnp.tril(jnp.ones((GLA_CHUNK, GLA_CHUNK), dtype=bool))
    def step(s, inp):
        qc, kc, vc, bc = inp
        bl = bc[..., -1:, :]
        qd = qc * jnp.exp(bc)
        kd = kc * jnp.exp(-bc)
        att = jnp.where(mask, jnp.einsum('bhid,bhjd->bhij', qd, kd), 0.0)
        o = jnp.einsum('bhid,bhdv->bhiv', qd, s) + jnp.einsum('bhij,bhjv->bhiv', att, vc)
        s = s * jnp.exp(bl)[..., 0, :, None] + jnp.einsum('bhjd,bhjv->bhdv', kc * jnp.exp(bl - bc), vc)
        return s, o
    s, o = lax.scan(step, s0, (chunks(q), chunks(k), chunks(v), b))
    return jnp.moveaxis(o, 0, 2).reshape(bn, h, t, dv), s


def _even_mixer(z, s0_f, s0_b, gate_w, gate_b, gla_g):
    bn, t, _ = z.shape
    z_f, zq, zk, zv, zr, zgf, zgb = _split(z, EVEN_SPLITS)
    y_f = _fourier_mix(z_f)
    def heads(a, d):
        return a.astype(jnp.float32).reshape(bn, t, GLA_HEADS, d).transpose(0, 2, 1, 3)
    q = heads(zq, GLA_DK) * (GLA_DK ** -0.5)
    k = heads(zk, GLA_DK)
    v = heads(zv, GLA_DV)
    gw = gate_w.astype(jnp.float32)
    gb = gate_b.astype(jnp.float32)
    logg_f = heads(jax.nn.log_sigmoid(zgf.astype(jnp.float32) @ gw[0] + gb[0]), GLA_DK) / GLA_TAU
    logg_b = heads(jax.nn.log_sigmoid(zgb.astype(jnp.float32) @ gw[1] + gb[1]), GLA_DK) / GLA_TAU
    flip = lambda a: jnp.flip(a, axis=2)
    o_f, s_f = _gla_chunked(q, k, v, logg_f, s0_f)
    o_b, s_b = _gla_chunked(flip(q), flip(k), flip(v), flip(logg_b), s0_b)
    o = (o_f + flip(o_b)).transpose(0, 2, 1, 3)
    o = _rmsnorm(o, gla_g) * jax.nn.silu(zr.astype(jnp.float32).reshape(bn, t, GLA_HEADS, GLA_DV))
    y_g = o.reshape(bn, t, GLA_HEADS * GLA_DV).astype(z.dtype)
    return jnp.concatenate([y_f, y_g], axis=-1), s_f, s_b


def _short_conv(zb, zc, zx, w, b):
    u = zc * zx
    t = u.shape[1]
    up = jnp.pad(u, ((0, 0), (1, 1), (0, 0)))
    y = up[:, 0:t] * w[0] + up[:, 1:t + 1] * w[1] + up[:, 2:t + 2] * w[2] + b
    return zb * y


def _odd_parts(z):
    bn, t, _ = z.shape
    zb, zc, zx, q, k, v = _split(z, ODD_SPLITS)
    q = q.reshape(bn, t, DIFF_HEADS, 2, HEAD_DIM)
    k = k.reshape(bn, t, DIFF_HEADS, 2, HEAD_DIM)
    v = v.reshape(bn, t, DIFF_HEADS, DIFF_DV)
    return zb, zc, zx, q, k, v


def _diff_attn(q, k, v, lam):
    s = jnp.einsum('bqhsd,bkhsd->bhsqk', q, k, preferred_element_type=jnp.float32) * (HEAD_DIM ** -0.5)
    p = jax.nn.softmax(s, axis=-1)
    a = p[:, :, 0] - lam * p[:, :, 1]
    return jnp.einsum('bhqk,bkhv->bqhv', a, v.astype(jnp.float32))


def _diff_attn_blocked(q, k, v, lam):
    bn, t = q.shape[:2]
    nb = t // Q_BLOCK
    qb = jnp.moveaxis(q.reshape(bn, nb, Q_BLOCK, DIFF_HEADS, 2, HEAD_DIM), 1, 0)
    o = lax.map(lambda blk: _diff_attn(blk, k, v, lam), qb)
    return jnp.moveaxis(o, 0, 1).reshape(bn, t, DIFF_HEADS, DIFF_DV)


def _subln(o, g, lam_init, dtype):
    bn, t = o.shape[:2]
    return (_rmsnorm(o, g) * (1.0 - lam_init)).reshape(bn, t, DIFF_HEADS * DIFF_DV).astype(dtype)


def _odd_mixer(zl, zc, cos, sin, conv_w, conv_b, lam, lam_init, dnorm, with_ctx_out):
    lat = _odd_parts(zl)
    cx = _odd_parts(zc)
    ql = _axial_rope(lat[3], cos, sin)
    kl = _axial_rope(lat[4], cos, sin)
    k_all = jnp.concatenate([kl, cx[4]], axis=1)
    v_all = jnp.concatenate([lat[5], cx[5]], axis=1)
    att_l = _diff_attn_blocked(ql, k_all, v_all, lam)
    y_l = jnp.concatenate([_short_conv(lat[0], lat[1], lat[2], conv_w, conv_b),
                           _subln(att_l, dnorm, lam_init, zl.dtype)], axis=-1)
    if not with_ctx_out:
        return y_l, None
    att_c = _diff_attn(cx[3], cx[4], cx[5], lam)
    y_c = jnp.concatenate([_short_conv(cx[0], cx[1], cx[2], conv_w, conv_b),
                           _subln(att_c, dnorm, lam_init, zc.dtype)], axis=-1)
    return y_l, y_c


def setup_inputs(seed: int = 0) -> dict:
    key = jax.random.key(seed)
    ks = jax.random.split(key, 28)
    nrm = lambda k, shape, s: jax.random.normal(k, shape, jnp.float32) * s
    gain = lambda k, shape: 1.0 + 0.05 * jax.random.normal(k, shape, jnp.float32)
    D = D_MODEL
    return {
        'x': nrm(ks[0], (BATCH, SEQ, D), 1.0),
        'c': nrm(ks[1], (BATCH, D), 1.0),
        'ctx': nrm(ks[2], (BATCH, CTX_LEN, D), 1.0),
        'c_ctx': nrm(ks[3], (D,), 1.0),
        'ada_w': nrm(ks[4], (DEPTH, D, N_MOD * D), 0.5 * D ** -0.5),
        'ada_b': nrm(ks[5], (DEPTH, N_MOD * D), 0.01),
        'norm_ffn1': gain(ks[6], (DEPTH, D)),
        'norm_mix': gain(ks[7], (DEPTH, D)),
        'norm_ffn2': gain(ks[8], (DEPTH, D)),
        'ffn1_w_in': nrm(ks[9], (DEPTH, D, 2 * FFN_HIDDEN), D ** -0.5),
        'ffn1_w_out': nrm(ks[10], (DEPTH, FFN_HIDDEN, D), FFN_HIDDEN ** -0.5),
        'ffn2_w_in': nrm(ks[11], (DEPTH, D, 2 * FFN_HIDDEN), D ** -0.5),
        'ffn2_w_out': nrm(ks[12], (DEPTH, FFN_HIDDEN, D), FFN_HIDDEN ** -0.5),
        'mix_w_out': nrm(ks[13], (DEPTH, MIX_WIDTH, D), MIX_WIDTH ** -0.5),
        'even_w_in': nrm(ks[14], (N_EVEN, D, EVEN_IN), D ** -0.5),
        'gla_gate_w': nrm(ks[15], (N_EVEN, 2, GLA_GATE_RANK, GLA_HEADS * GLA_DK), GLA_GATE_RANK ** -0.5),
        'gla_gate_b': nrm(ks[16], (N_EVEN, 2, GLA_HEADS * GLA_DK), 0.1),
        'gla_norm': gain(ks[17], (N_EVEN, GLA_DV)),
        'odd_w_in': nrm(ks[18], (N_ODD, D, ODD_IN), D ** -0.5),
        'conv_w': nrm(ks[19], (N_ODD, CONV_TAPS, CONV_WIDTH), CONV_TAPS ** -0.5),
        'conv_b': nrm(ks[20], (N_ODD, CONV_WIDTH), 0.01),
        'lambda_q1': nrm(ks[21], (N_ODD, HEAD_DIM), 0.1),
        'lambda_k1': nrm(ks[22], (N_ODD, HEAD_DIM), 0.1),
        'lambda_q2': nrm(ks[23], (N_ODD, HEAD_DIM), 0.1),
        'lambda_k2': nrm(ks[24], (N_ODD, HEAD_DIM), 0.1),
        'diff_norm': gain(ks[25], (N_ODD, DIFF_DV)),
        'final_norm': gain(ks[26], (D,)),
    }


def reference(x, c, ctx, c_ctx, ada_w, ada_b, norm_ffn1, norm_mix, norm_ffn2, ffn1_w_in, ffn1_w_out,
              ffn2_w_in, ffn2_w_out, mix_w_out, even_w_in, gla_gate_w, gla_gate_b, gla_norm, odd_w_in,
              conv_w, conv_b, lambda_q1, lambda_k1, lambda_q2, lambda_k2, diff_norm, final_norm):
    bn = x.shape[0]
    rows = x.shape[1] // GRID_W
    cos, sin = _axial_rope_tables(rows)
    h, hc = x, ctx
    for layer in range(DEPTH):
        last = layer == DEPTH - 1
        sh1, sc1, g1, shm, scm, gm, sh2, sc2, g2 = _modulation(c, ada_w[layer], ada_b[layer])
        csh1, csc1, cg1, cshm, cscm, cgm, csh2, csc2, cg2 = _modulation(c_ctx, ada_w[layer], ada_b[layer])
        h = h + 0.5 * g1 * _swiglu(_prenorm(h, norm_ffn1[layer], sh1, sc1), ffn1_w_in[layer], ffn1_w_out[layer])
        hc = hc + 0.5 * cg1 * _swiglu(_prenorm(hc, norm_ffn1[layer], csh1, csc1), ffn1_w_in[layer], ffn1_w_out[layer])
        if layer % 2 == 0:
            i = layer // 2
            w_in = even_w_in[i]
            zl = _prenorm(h, norm_mix[layer], shm, scm) @ w_in
            zc = _prenorm(hc, norm_mix[layer], cshm, cscm) @ w_in
            s0 = jnp.zeros((bn, GLA_HEADS, GLA_DK, GLA_DV), jnp.float32)
            mix_c, s_f, s_b = _even_mixer(zc, s0, s0, gla_gate_w[i], gla_gate_b[i], gla_norm[i])
            mix_l, _, _ = _even_mixer(zl, s_f, s_b, gla_gate_w[i], gla_gate_b[i], gla_norm[i])
        else:
            i = layer // 2
            w_in = odd_w_in[i]
            zl = _prenorm(h, norm_mix[layer], shm, scm) @ w_in
            zc = _prenorm(hc, norm_mix[layer], cshm, cscm) @ w_in
            lam_init = 0.8 - 0.6 * math.exp(-0.3 * layer)
            lam = (jnp.exp(jnp.sum(lambda_q1[i].astype(jnp.float32) * lambda_k1[i].astype(jnp.float32)))
                   - jnp.exp(jnp.sum(lambda_q2[i].astype(jnp.float32) * lambda_k2[i].astype(jnp.float32)))
                   + lam_init)
            mix_l, mix_c = _odd_mixer(zl, zc, cos, sin, conv_w[i], conv_b[i], lam, lam_init, diff_norm[i],
                                      not last)
        h = h + gm * (mix_l @ mix_w_out[layer])
        h = h + 0.5 * g2 * _swiglu(_prenorm(h, norm_ffn2[layer], sh2, sc2), ffn2_w_in[layer], ffn2_w_out[layer])
        if not last:
            hc = hc + cgm * (mix_c @ mix_w_out[layer])
            hc = hc + 0.5 * cg2 * _swiglu(_prenorm(hc, norm_ffn2[layer], csh2, csc2), ffn2_w_in[layer], ffn2_w_out[layer])
    return _rmsnorm(h, final_norm)
```

```python
import math
from contextlib import ExitStack
import numpy as np
import concourse.bass as bass
import concourse.mybir as mybir
from concourse.bass_utils import run_bass_kernel_spmd

F32 = mybir.dt.float32
F32R = mybir.dt.float32r
AF = mybir.ActivationFunctionType
ALU = mybir.AluOpType

D = 1024
TL = 2048
TC = 256
T = TL + TC
HID = 2816
NJ = HID // 128
EPS = 1e-6
LAM_INIT = 0.8 - 0.6 * math.exp(-0.3 * 1)
NF = 27600
NR = 25600


class Op:
    __slots__ = ("eng", "fn", "dma", "deps", "needs_inc", "ticket", "idx")


class Prog:
    ENGS = ("pe", "act", "dve", "pool", "sp")

    def __init__(self):
        self.ops = []
        self.last_w = {}
        self.readers = {}

    def add(self, eng, fn, reads=(), writes=(), dma=False, barrier=False):
        op = Op()
        op.eng, op.fn, op.dma, op.needs_inc, op.ticket = eng, fn, dma, False, None
        op.idx = len(self.ops)
        hard, war = set(), set()
        for k in reads:
            w = self.last_w.get(k)
            if w is not None:
                hard.add(w)
        for k in writes:
            w = self.last_w.get(k)
            if w is not None:
                hard.add(w)
            for r in self.readers.get(k, ()):
                war.add(r)
        deps = []
        for d in hard | war:
            if d is op:
                continue
            if not d.dma and not dma and d.eng == eng and not barrier:
                if eng == "pe":
                    continue
                if d not in hard:
                    continue
            deps.append(d)
        for d in deps:
            d.needs_inc = True
        op.deps = deps
        for k in reads:
            self.readers.setdefault(k, []).append(op)
        for k in writes:
            self.last_w[k] = op
            self.readers[k] = []
        self.ops.append(op)
        return op

    def barrier(self):
        allk = list(self.last_w.keys() | self.readers.keys())
        for e in self.ENGS:
            self.add(e, None, writes=allk + ["__barrier__"], barrier=True)
        self.last_w = {}
        self.readers = {}

    def emit(self, nc, block, sems, dma_sems):
        cnt = {e: 0 for e in self.ENGS}
        dcnt = {}
        dnext = {e: 0 for e in self.ENGS}
        dprev = {}
        for op in self.ops:
            if op.dma:
                pool = dma_sems[op.eng]
                s = pool[dnext[op.eng] % len(pool)]
                dnext[op.eng] += 1
                prev = dcnt.get(id(s), 0)
                dprev[op.idx] = (s, prev)
                dcnt[id(s)] = prev + 16
                op.ticket = (s, prev + 16)
            elif op.needs_inc:
                cnt[op.eng] += 1
                op.ticket = (sems[op.eng], cnt[op.eng])
        per = {e: [o for o in self.ops if o.eng == e] for e in self.ENGS}

        def body(eng_name):
            def run(e):
                known = {}
                for op in per[eng_name]:
                    waits = {}
                    for d in op.deps:
                        s, v = d.ticket
                        if known.get(id(s), 0) < v and waits.get(id(s), (None, 0))[1] < v:
                            waits[id(s)] = (s, v)
                    if op.dma:
                        s, v = dprev[op.idx]
                        if v > 0 and known.get(id(s), 0) < v and waits.get(id(s), (None, 0))[1] < v:
                            waits[id(s)] = (s, v)
                    for s, v in waits.values():
                        e.wait_ge(s, v)
                        known[id(s)] = v
                    if op.fn is None:
                        if op.needs_inc:
                            e.nop().then_inc(op.ticket[0], 1)
                        continue
                    ins = op.fn(e)
                    if op.dma:
                        ins.then_inc(op.ticket[0], 16)
                    elif op.needs_inc:
                        ins.then_inc(op.ticket[0], 1)
            return run

        block.tensor(body("pe"))
        block.scalar(body("act"))
        block.vector(body("dve"))
        block.gpsimd(body("pool"))
        block.sync(body("sp"))


def seg_cols(t0, tn):
    out = []
    a, b = t0, min(t0 + tn, TL)
    if b > a:
        out.append((0, 0, a, b - a))
    a2, b2 = max(t0, TL), t0 + tn
    if b2 > a2:
        out.append((1, a2 - t0, a2, b2 - a2))
    return out


def halves_of(t_end):
    return [(0, 1024), (1024, t_end)]


HMAX = 1280


def blocks_of(t0, t1, bs=512):
    out = []
    t = t0
    while t < t1:
        n = min(bs, t1 - t)
        out.append((t, n))
        t += n
    return out


class Builder:
    def __init__(self, stop_after=None):
        self.stop_after = stop_after
        self.nc = bass.Bass("TRN2", target_bir_lowering=False)
        self.P = Prog()
        self.es = ExitStack()
        self.bank_i = 0
        self.uid = 0

    def dram_in(self, name, shape):
        return self.nc.dram_tensor(name, list(shape), F32, kind="ExternalInput").ap()

    def carve(self, n, dtype=F32):
        if dtype is F32R:
            assert self.topr + n <= NR, (self.topr, n)
            ap = self.arenaR[:, self.topr:self.topr + n]
            self.topr += n
            return ap
        assert self.top + n <= NF, (self.top, n)
        ap = self.arena[:, self.top:self.top + n]
        self.top += n
        return ap

    def reset(self):
        self.top = self.base_top
        self.topr = self.base_topr

    def bank(self, pool):
        lst = self.pools[pool]
        i = self.pool_i.get(pool, 0)
        self.pool_i[pool] = i + 1
        b = lst[i % len(lst)]
        return self.ps[b], ("ps", b)

    def mm(self, out, lhsT, rhs, start, stop, reads, writes):
        self.P.add("pe", lambda e: e.matmul(out, lhsT, rhs, start=start, stop=stop), reads=reads, writes=writes)

    def act(self, out, in_, func, reads, writes, bias=None, scale=None):
        kw = {}
        if bias is not None:
            kw["bias"] = bias
        if scale is not None:
            kw["scale"] = scale
        self.P.add("act", lambda e: e.activation(out=out, in_=in_, func=func, **kw), reads=reads, writes=writes)

    def dve(self, fn, reads, writes):
        self.P.add("dve", fn, reads=reads, writes=writes)

    def dma(self, q, out, in_, reads, writes):
        self.P.add(q, lambda e: e.dma_start(out=out, in_=in_), reads=reads, writes=writes, dma=True)

    def build(self):
        nc, P, es = self.nc, self.P, self.es
        I = {}
        I["x"] = self.dram_in("x", (TL, D))
        I["ctx"] = self.dram_in("ctx", (TC, D))
        I["cond"] = self.dram_in("cond", (128, 16))
        I["ada_w"] = self.dram_in("ada_w", (2, D, 9 * D))
        I["ada_b"] = self.dram_in("ada_b", (128, 2 * 72))
        I["gains"] = self.dram_in("gains", (128, 7 * 8))
        I["ffn1_w_in"] = self.dram_in("ffn1_w_in", (2, D, 2 * HID))
        I["ffn1_w_out"] = self.dram_in("ffn1_w_out", (2, HID, D))
        I["ffn2_w_in"] = self.dram_in("ffn2_w_in", (2, D, 2 * HID))
        I["ffn2_w_out"] = self.dram_in("ffn2_w_out", (2, HID, D))
        I["ident"] = self.dram_in("ident", (128, 128))
        I["mix_w_out"] = self.dram_in("mix_w_out", (2, D, D))
        I["even_w_in"] = self.dram_in("even_w_in", (D, 2592))
        I["dft_lat"] = self.dram_in("dft_lat", (2, TL, TL))
        I["dft_ctx"] = self.dram_in("dft_ctx", (2, TC, TC))
        I["bd64"] = self.dram_in("bd64", (128, 256))
        I["gla_c"] = self.dram_in("gla_c", (64, 6 * 64))
        I["gw_pad"] = self.dram_in("gw_pad", (33, 768))
        I["ones_row"] = self.dram_in("ones_row", (1, T))
        I["vecs128"] = self.dram_in("vecs128", (128, 8))
        I["odd_w_in"] = self.dram_in("odd_w_in", (D, 3072))
        I["rope"] = self.dram_in("rope", (2, 128, TL))
        I["permT"] = self.dram_in("permT", (128, 128))
        I["lamv"] = self.dram_in("lamv", (64, 4))
        I["convv"] = self.dram_in("convv", (128, 8))
        I["fnorm_bc"] = self.dram_in("fnorm_bc", (128, D))
        self.I = I
        skind = "ExternalOutput" if self.stop_after is not None else "Internal"
        self.zT = nc.dram_tensor("zT", [3072, T], F32, kind=skind).ap()
        self.vtok = nc.dram_tensor("vtok", [T, 768], F32, kind=skind).ap()
        self.ktok = nc.dram_tensor("ktok", [T, 384], F32, kind=skind).ap()
        self.mixT = nc.dram_tensor("mixT", [D, T], F32, kind=skind).ap()
        if self.stop_after is not None:
            self.dbg = nc.dram_tensor("dbg", [128, 8 * T], F32, kind="ExternalOutput").ap()
        else:
            self.out = nc.dram_tensor("out", [TL, D], F32, kind="ExternalOutput").ap()

        self.arena = es.enter_context(nc.sbuf_tensor("arena", [128, NF], F32))
        self.arenaR = es.enter_context(nc.sbuf_tensor("arenaR", [128, NR], F32R))
        self.ps = [es.enter_context(nc.psum_tensor(f"ps{i}", [128, 512], F32)) for i in range(8)]
        sems = {e: es.enter_context(nc.semaphore(f"s_{e}")) for e in Prog.ENGS}
        dma_sems = {e: [es.enter_context(nc.semaphore(f"d_{e}{i}")) for i in range(8)] for e in ("sp", "pool", "act")}
        dma_sems["pe"] = dma_sems["dve"] = []
        self.pools = {"a": [0, 1, 2, 3], "b": [4, 5, 6, 7], "all": list(range(8))}
        self.pool_i = {}

        self.top = 0
        self.topr = 0
        self.hT = self.carve(8 * T).rearrange("p (k t) -> p k t", k=8)
        self.ident = self.carve(128)
        self.ones_r = self.carve(128, F32R)
        self.ones_f = self.carve(128)
        self.cond = self.carve(16).rearrange("p (k c) -> p k c", c=2)
        self.sc = self.carve(16, F32R).rearrange("p (k c) -> p k c", c=2)
        self.adab = self.carve(144).rearrange("p (l j) -> p l j", l=2)
        self.gains = self.carve(56).rearrange("p (g k) -> p g k", k=8)
        self.mod = self.carve(144).rearrange("p (j c) -> p j c", c=2)
        self.mA = self.carve(48).rearrange("p (i k c) -> p i k c", i=3, k=8)
        self.mG = self.carve(48).rearrange("p (i k c) -> p i k c", i=3, k=8)
        self.eps_ap = self.carve(1)
        self.one_ap = self.carve(1)
        self.vecs = self.carve(8)
        self.glac = self.carve(6 * 64)
        self.glacR = self.carve(6 * 64, F32R)
        self.bd64 = self.carve(256, F32R)
        self.gw = self.carve(768, F32R)
        self.permT = self.carve(128, F32R)
        self.convv = self.carve(8)
        self.lamv = self.carve(4)
        self.lamp = self.carve(2, F32R)
        self.lame = self.carve(2)
        self.neglam = self.carve(1)
        self.base_top = self.top
        self.base_topr = self.topr

        self.phase_setup()
        self.phase_load()
        for layer in range(2):
            self.phase_mod(layer)
            self.phase_ffn(layer, 0, T)
            if self.stop_after == f"ffn1_{layer}":
                break
            if layer == 0:
                fm = [(o * 128, 128, o * 128) for o in list(range(0, 8)) + list(range(14, 20))] + [(2560, 32, 2560)]
                tok = [(1024, 512, self.vtok[:, 0:512]), (1536, 256, self.vtok[:, 512:768]), (640, 384, self.ktok)]
                self.phase_proj(layer, self.I["even_w_in"], fm, tok)
                if self.stop_after == "proj_0":
                    break
                self.phase_fourier()
                if self.stop_after == "fourier":
                    break
                self.phase_gla()
                if self.stop_after == "gla":
                    break
            else:
                fm = [(o * 128, 128, o * 128) for o in range(18)]
                tok = [(2304, 512, self.vtok[:, 0:512]), (2816, 256, self.vtok[:, 512:768])]
                self.phase_proj(layer, self.I["odd_w_in"], fm, tok)
                if self.stop_after == "proj_1":
                    break
                self.phase_conv()
                self.phase_dattn()
                if self.stop_after == "dattn":
                    break
            self.phase_outproj(layer, T if layer == 0 else TL)
            if self.stop_after == f"mix_{layer}":
                break
            self.phase_ffn(layer, 2, T if layer == 0 else TL)
            if self.stop_after == f"ffn2_{layer}":
                break
        if self.stop_after is not None:
            self.phase_dump()
        else:
            self.phase_final()

        with nc.Block() as block:
            P.emit(nc, block, sems, dma_sems)
        return nc

    def phase_setup(self):
        I = self.I
        self.dma("sp", self.ident, I["ident"], [], ["ident"])
        self.dma("sp", self.cond.rearrange("p k c -> p (k c)"), I["cond"], [], ["cond"])
        self.dma("sp", self.adab.rearrange("p l j -> p (l j)"), I["ada_b"], [], ["adab"])
        self.dma("sp", self.gains.rearrange("p g k -> p (g k)"), I["gains"], [], ["gains"])
        ones_r = self.ones_r
        ones_f = self.ones_f
        self.dve(lambda e: e.memset(ones_f, 1.0), [], ["ones_f"])
        self.dve(lambda e: e.tensor_copy(out=ones_r, in_=ones_f), ["ones_f"], ["ones"])
        eps_ap = self.eps_ap
        self.dve(lambda e: e.memset(eps_ap, EPS), [], ["eps"])
        one_ap = self.one_ap
        self.dve(lambda e: e.memset(one_ap, 1.0), [], ["one"])
        self.dma("sp", self.vecs, I["vecs128"], [], ["vecs"])
        self.dma("sp", self.glac[0:64, :], I["gla_c"], [], ["glac"])
        self.dma("pool", self.glacR[0:64, :], I["gla_c"], [], ["glacR"])
        self.dma("pool", self.bd64, I["bd64"], [], ["bd64"])
        self.dma("pool", self.gw[0:33, :], I["gw_pad"], [], ["gw"])
        self.dma("pool", self.permT, I["permT"], [], ["permT"])
        self.dma("sp", self.convv, I["convv"], [], ["convv"])
        self.dma("sp", self.lamv[0:64, :], I["lamv"], [], ["lamv"])
        lamv, lamp, lame, neglam, vecs = self.lamv, self.lamp, self.lame, self.neglam, self.vecs
        self.dve(lambda e: e.tensor_tensor(out=lamp[0:64, :], in0=lamv[0:64, 0:4:2], in1=lamv[0:64, 1:4:2], op=ALU.mult),
                 ["lamv"], ["lamp"])
        pb, pk = self.bank("all")
        self.mm(pb[:, 0:2], self.ones_r[0:64, :], lamp[0:64, :], True, True, ["ones", "lamp"], [pk])
        self.act(lame, pb[:, 0:2], AF.Exp, [pk], ["lame"])
        self.dve(lambda e: e.tensor_tensor(out=neglam, in0=lame[:, 1:2], in1=lame[:, 0:1], op=ALU.subtract), ["lame"], ["neglam0"])
        self.dve(lambda e: e.tensor_scalar(out=neglam, in0=neglam, scalar1=-LAM_INIT, scalar2=None, op0=ALU.add),
                 ["neglam0"], ["neglam"])
        self.dve(lambda e: e.tensor_scalar(out=vecs[:, 2:3], in0=vecs[:, 1:2], scalar1=1.0 - LAM_INIT, scalar2=None, op0=ALU.mult),
                 ["vecs"], ["vecs2"])
        self.act(self.sc, self.cond, AF.Silu, ["cond"], ["sc"])

    def phase_load(self):
        self.reset()
        xin = [self.carve(D) for _ in range(2)]
        hT = self.hT
        for tt in range(T // 128):
            s = tt % 2
            src = self.I["x"][tt * 128:(tt + 1) * 128, :] if tt < 16 else self.I["ctx"][(tt - 16) * 128:(tt - 15) * 128, :]
            self.dma("sp", xin[s], src, [], [("xin", s)])
            for g in range(2):
                pb, pk = self.bank("all")
                for kk in range(4):
                    k = g * 4 + kk
                    o = pb[:, kk * 128:(kk + 1) * 128]
                    i_ = xin[s][:, k * 128:(k + 1) * 128]
                    idn = self.ident
                    self.P.add("pe", lambda e, o=o, i_=i_, idn=idn: e.transpose(o, i_, idn),
                               reads=[("xin", s), "ident"], writes=[pk])
                dst = hT[:, g * 4:(g + 1) * 4, tt * 128:(tt + 1) * 128]
                srcp = pb.rearrange("p (k t) -> p k t", k=4)
                if g == 0:
                    self.dve(lambda e, dst=dst, srcp=srcp: e.tensor_copy(out=dst, in_=srcp), [pk], [("hw", tt, g)])
                else:
                    self.act(dst, srcp, AF.Copy, [pk], [("hw", tt, g)])
        self.P.barrier()

    def phase_mod(self, layer):
        self.reset()
        wb = [self.carve(4096, F32R).rearrange("p (k n) -> p k n", k=8) for _ in range(2)]
        aw = self.I["ada_w"][layer].rearrange("(k p) n -> p k n", p=128)
        pb, pk = self.bank("all")
        for cb in range(18):
            s = cb % 2
            self.dma("pool", wb[s], aw[:, :, cb * 512:(cb + 1) * 512], [], [("wb", s)])
            for jj in range(4):
                j = cb * 4 + jj
                for k in range(8):
                    self.mm(pb[:, 2 * j:2 * j + 2], wb[s][:, k, jj * 128:(jj + 1) * 128], self.sc[:, k, :],
                            k == 0, k == 7, [("wb", s), "sc"], [pk])
        mod, adab = self.mod, self.adab
        pm = pb[:, 0:144].rearrange("p (j c) -> p j c", c=2)
        for c in range(2):
            self.dve(lambda e, c=c: e.tensor_tensor(out=mod[:, :, c], in0=pm[:, :, c], in1=adab[:, layer, :], op=ALU.add),
                     [pk, "adab"], [("mod", c)])
        mA, mG, gains = self.mA, self.mG, self.gains
        for i in range(3):
            for c in range(2):
                sc_i = mod[:, (3 * i + 1) * 8:(3 * i + 2) * 8, c]
                g_i = mod[:, (3 * i + 2) * 8:(3 * i + 3) * 8, c]
                gn = gains[:, i * 2 + layer, :]
                self.dve(lambda e, i=i, c=c, sc_i=sc_i, gn=gn: e.scalar_tensor_tensor(
                    out=mA[:, i, :, c], in0=sc_i, scalar=1.0, in1=gn, op0=ALU.add, op1=ALU.mult),
                    [("mod", c), "gains"], [("mA", i, c)])
                fac = 1.0 if i == 1 else 0.5
                self.dve(lambda e, i=i, c=c, g_i=g_i, fac=fac: e.tensor_scalar(
                    out=mG[:, i, :, c], in0=g_i, scalar1=fac, scalar2=None, op0=ALU.mult),
                    [("mod", c)], [("mG", i, c)])
        self.P.barrier()

    def prenorm(self, i, blks, xn, xkey):
        hT, mA, mod = self.hT, self.mA, self.mod
        base = blks[0][0]
        sq = [self.carve(512, F32R) for _ in range(2)]
        tmp = [self.carve(512) for _ in range(2)]
        rt = self.carve(512)
        rr = self.carve(512)
        ones = self.ones_r
        n_sq = 0
        n_tmp = 0
        for bi, (t0, tn) in enumerate(blks):
            pb, pk = self.bank("a")
            for k in range(8):
                s = n_sq % 2
                n_sq += 1
                self.act(sq[s][:, :tn], hT[:, k, t0:t0 + tn], AF.Square, [("h", k, bi)], [("sq", s)])
                self.mm(pb[:, :tn], ones, sq[s][:, :tn], k == 0, k == 7, ["ones", ("sq", s)], [pk])
            self.act(rt[:, :tn], pb[:, :tn], AF.Sqrt, [pk], ["rt"], bias=self.eps_ap, scale=1.0 / D)
            self.dve(lambda e, tn=tn: e.reciprocal(out=rr[:, :tn], in_=rt[:, :tn]), ["rt"], ["rr"])
            for k in range(8):
                for (c, off, a, n) in seg_cols(t0, tn):
                    s = n_tmp % 2
                    n_tmp += 1
                    tm = tmp[s][:, :n]
                    self.dve(lambda e, tm=tm, k=k, a=a, n=n, c=c, off=off: e.scalar_tensor_tensor(
                        out=tm, in0=hT[:, k, a:a + n], scalar=mA[:, i, k, c:c + 1], in1=rr[:, off:off + n],
                        op0=ALU.mult, op1=ALU.mult), [("h", k, bi), "rr", ("mA", i, c)], [("tmp", s)])
                    self.act(xn[:, k, a - base:a - base + n], tm, AF.Identity, [("tmp", s), ("mod", c)],
                             [(xkey, k, bi)], bias=mod[:, 3 * i * 8 + k, c:c + 1])

    def phase_ffn(self, layer, i, t_end):
        I = self.I
        w_in = I["ffn1_w_in" if i == 0 else "ffn2_w_in"][layer].rearrange("(k p) n -> p k n", p=128)
        w_out = I["ffn1_w_out" if i == 0 else "ffn2_w_out"][layer]
        hT, mG = self.hT, self.mG
        half = HMAX
        for (h0, h1) in halves_of(t_end):
            self.reset()
            blks = blocks_of(h0, h1)
            base = blks[0][0]
            xn = self.carve(8 * half, F32R).rearrange("p (k t) -> p k t", k=8)
            actb = [self.carve(half, F32R) for _ in range(2)]
            wb = [self.carve(3072, F32R) for _ in range(3)]
            sg = [self.carve(512) for _ in range(2)]
            evb = [self.carve(512) for _ in range(4)]
            n_ev = 0
            self.prenorm(i, blks, xn, "xn")
            n_sg = 0

            def load_w(j):
                s = j % 3
                self.dma("pool", wb[s][:, 0:1024].rearrange("p (k n) -> p k n", k=8), w_in[:, :, j * 128:(j + 1) * 128],
                         [], [("wg", s)])
                self.dma("pool", wb[s][:, 1024:2048].rearrange("p (k n) -> p k n", k=8),
                         w_in[:, :, HID + j * 128:HID + (j + 1) * 128], [], [("wu", s)])
                self.dma("pool", wb[s][:, 2048:3072], w_out[j * 128:(j + 1) * 128, :], [], [("wo", s)])

            def phase_a(j, bi):
                nonlocal n_sg
                s = j % 3
                a_s = j % 2
                t0, tn = blks[bi]
                pg, pgk = self.bank("a")
                pu, puk = self.bank("a")
                lo = t0 - base
                for k in range(8):
                    self.mm(pg[:, :tn], wb[s][:, k * 128:(k + 1) * 128], xn[:, k, lo:lo + tn], k == 0, k == 7,
                            [("wg", s), ("xn", k, bi)], [pgk])
                for k in range(8):
                    self.mm(pu[:, :tn], wb[s][:, 1024 + k * 128:1024 + (k + 1) * 128], xn[:, k, lo:lo + tn],
                            k == 0, k == 7, [("wu", s), ("xn", k, bi)], [puk])
                q = n_sg % 2
                n_sg += 1
                self.act(sg[q][:, :tn], pg[:, :tn], AF.Silu, [pgk], [("sg", q)])
                dst = actb[a_s][:, lo:lo + tn]
                self.dve(lambda e, dst=dst, q=q, tn=tn, pu=pu: e.tensor_tensor(
                    out=dst, in0=sg[q][:, :tn], in1=pu[:, :tn], op=ALU.mult), [("sg", q), puk], [("act", a_s, bi)])

            def phase_b(j, bi):
                nonlocal n_ev
                s = j % 3
                a_s = j % 2
                t0, tn = blks[bi]
                lo = t0 - base
                for f in range(8):
                    py, pyk = self.bank("b")
                    self.mm(py[:, :tn], wb[s][:, 2048 + f * 128:2048 + (f + 1) * 128], actb[a_s][:, lo:lo + tn],
                            True, True, [("wo", s), ("act", a_s, bi)], [pyk])
                    for (c, off, a, n) in seg_cols(t0, tn):
                        if f % 2 == 0:
                            self.dve(lambda e, py=py, f=f, a=a, n=n, c=c, off=off: e.scalar_tensor_tensor(
                                out=hT[:, f, a:a + n], in0=py[:, off:off + n], scalar=mG[:, i, f, c:c + 1],
                                in1=hT[:, f, a:a + n], op0=ALU.mult, op1=ALU.add),
                                [pyk, ("mG", i, c), ("h", f, bi)], [("h", f, bi)])
                        else:
                            q = n_ev % 4
                            n_ev += 1
                            tq = evb[q][:, :n]
                            self.act(tq, py[:, off:off + n], AF.Copy, [pyk, ("mG", i, c)], [("evb", q)], scale=mG[:, i, f, c:c + 1])
                            self.P.add("pool", lambda e, tq=tq, f=f, a=a, n=n: e.tensor_tensor(
                                out=hT[:, f, a:a + n], in0=tq, in1=hT[:, f, a:a + n], op=ALU.add),
                                reads=[("evb", q), ("h", f, bi)], writes=[("h", f, bi)])

            load_w(0)
            load_w(1)
            for bi in range(len(blks)):
                phase_a(0, bi)
            for j in range(NJ):
                if j + 2 < NJ:
                    load_w(j + 2)
                for bi in range(len(blks)):
                    if j + 1 < NJ:
                        phase_a(j + 1, bi)
                    phase_b(j, bi)
            self.P.barrier()

    def evac(self, n, out, in_, reads, writes):
        if n % 2 == 0:
            self.act(out, in_, AF.Copy, reads, writes)
        else:
            self.dve(lambda e: e.tensor_copy(out=out, in_=in_), reads, writes)

    def phase_proj(self, layer, w_ap, fm, tok):
        w_in = w_ap.rearrange("(k p) n -> p k n", p=128)
        half = HMAX
        ntokc = sum(n for _, n, _ in tok)
        for hb, (h0, h1) in enumerate(halves_of(T)):
            self.reset()
            blks = blocks_of(h0, h1)
            base = blks[0][0]
            xn = self.carve(8 * half, F32R).rearrange("p (k t) -> p k t", k=8)
            wtok = self.carve(8 * ntokc, F32R).rearrange("p (k n) -> p k n", k=8)
            wb = [self.carve(1024, F32R).rearrange("p (k n) -> p k n", k=8) for _ in range(2)]
            stg = [self.carve(512) for _ in range(4)]
            self.prenorm(1, blks, xn, "xn")
            c0 = 0
            tokc = []
            for (col0, n, dst) in tok:
                self.dma("pool", wtok[:, :, c0:c0 + n], w_in[:, :, col0:col0 + n], [], [("wtok", c0)])
                tokc.append((c0, n, dst))
                c0 += n
            ns = 0
            xkeys = [("xn", k, bi) for k in range(8) for bi in range(len(blks))]
            for ci, (col0, n, row0) in enumerate(fm):
                s = ci % 2
                self.dma("pool", wb[s][:, :, 0:n], w_in[:, :, col0:col0 + n], [], [("wb", s)])
                for bi, (t0, tn) in enumerate(blks):
                    pb, pk = self.bank("a")
                    lo = t0 - base
                    for k in range(8):
                        self.mm(pb[:n, :tn], wb[s][:, k, 0:n], xn[:, k, lo:lo + tn], k == 0, k == 7,
                                [("wb", s), ("xn", k, bi)], [pk])
                    q = ns % 4
                    ns += 1
                    self.evac(ns, stg[q][:n, :tn], pb[:n, :tn], [pk], [("stg", q)])
                    self.dma("sp", self.zT[row0:row0 + n, t0:t0 + tn], stg[q][:n, :tn], [("stg", q)], [("zT", ci, bi, hb)])
            for tt in range((h1 - h0) // 128):
                t0 = base + tt * 128
                for (c0, n, dst) in tokc:
                    pb, pk = self.bank("b")
                    for k in range(8):
                        self.mm(pb[:, :n], xn[:, k, tt * 128:(tt + 1) * 128], wtok[:, k, c0:c0 + n], k == 0, k == 7,
                                xkeys + [("wtok", c0)], [pk])
                    q = ns % 4
                    ns += 1
                    self.evac(ns, stg[q][:, :n], pb[:, :n], [pk], [("stg", q)])
                    self.dma("sp", dst[t0:t0 + 128, :], stg[q][:, :n], [("stg", q)], [("ztok", c0, tt, hb)])
            self.P.barrier()

    def phase_fourier(self):
        I = self.I
        for (t_off, Ts, tab) in ((TL, TC, I["dft_ctx"]), (0, TL, I["dft_lat"])):
            self.reset()
            ntt = Ts // 128
            zf = self.carve(2 * Ts, F32R).rearrange("p (c t) -> p c t", c=2)
            zcs = self.carve(ntt * 512, F32R).rearrange("p (t c n) -> p t c n", t=ntt, c=2)
            G = min(4, ntt)
            tb_ = [self.carve(G * 512, F32R).rearrange("p (g n) -> p g n", g=G) for _ in range(3)]
            stg = [self.carve(512) for _ in range(2)]
            for c in range(2):
                self.dma("pool", zf[:, c, :], self.zT[c * 128:(c + 1) * 128, t_off:t_off + Ts], [], [("zf", c)])
            ne = 0
            for tt in range(ntt):
                for c in range(2):
                    pb, pk = self.bank("a")
                    self.mm(pb[:, 0:256], zf[:, c, tt * 128:(tt + 1) * 128], self.bd64, True, True,
                            [("zf", c), "bd64"], [pk])
                    ne += 1
                    self.evac(ne, zcs[:, tt, c, :], pb[:, 0:256], [pk], [("zcs", tt, c)])
            ncb = max(1, Ts // 512)
            cw = min(512, Ts)
            nl = 0
            for cb in range(ncb):
                acc = [self.bank("b") for _ in range(2)]
                first = True
                ngr = ntt // G
                for cs in range(2):
                    for g in range(ngr):
                        s = nl % 3
                        nl += 1
                        src = tab[cs].rearrange("(t p) n -> p t n", p=128)[:, g * G:(g + 1) * G, cb * cw:(cb + 1) * cw]
                        self.dma("pool", tb_[s][:, :, :cw], src, [], [("tb", s)])
                        for gi in range(G):
                            tt = g * G + gi
                            last = (cs == 1 and g == ngr - 1 and gi == G - 1)
                            for c in range(2):
                                self.mm(acc[c][0][:, :cw], zcs[:, tt, c, cs * 128:(cs + 1) * 128], tb_[s][:, gi, :cw],
                                        first, last, [("zcs", tt, c), ("tb", s)], [acc[c][1]])
                            first = False
                for c in range(2):
                    q = (cb * 2 + c) % 2
                    ne += 1
                    self.evac(ne, stg[q][:, :cw], acc[c][0][:, :cw], [acc[c][1]], [("stg", q)])
                    self.dma("sp", self.mixT[c * 128:(c + 1) * 128, t_off + cb * cw:t_off + (cb + 1) * cw], stg[q][:, :cw],
                             [("stg", q)], [("mixT", c, cb, t_off)])
            self.P.barrier()

    def phase_gla(self):
        I = self.I
        NCH = T // 64
        order = {0: [32, 33, 34, 35] + list(range(32)), 1: [35, 34, 33, 32] + list(range(31, -1, -1))}
        step_of = {d: {n: i for i, n in enumerate(order[d])} for d in (0, 1)}
        glacR, glac = self.glacR, self.glac
        SX = [glacR[0:64, 0:64], glacR[0:64, 64:128]]
        TX = [glacR[0:64, 128:192], glacR[0:64, 192:256]]
        MK = [glac[0:64, 256:320], glac[0:64, 320:384]]
        self.reset()
        zg = self.carve(T, F32R)
        self.dma("pool", zg[0:32, :], self.zT[2560:2592, :], [], ["zg"])
        self.dma("pool", zg[32:33, :], I["ones_row"], [], ["zg1"])
        qT = self.carve(T, F32R)
        kT = self.carve(T, F32R)
        vt = self.carve(NCH * 128, F32R).rearrange("p (n d) -> p n d", d=128)
        kt = self.carve(NCH * 64, F32R).rearrange("p (n d) -> p n d", d=64)
        NS = 12
        LOOK = 4
        logg = [self.carve(64, F32R) for _ in range(NS)]
        kdec = [self.carve(64, F32R) for _ in range(NS)]
        qd = [self.carve(64, F32R) for _ in range(NS)]
        kd = [self.carve(64, F32R) for _ in range(NS)]
        attm = [self.carve(64, F32R) for _ in range(NS)]
        S = [[self.carve(128, F32R) for _ in range(2)] for _ in range(2)]
        sqr = [self.carve(512, F32R) for _ in range(2)]
        ef = [self.carve(64) for _ in range(NS)]
        spf = ef
        EDf = [self.carve(64) for _ in range(NS)]
        Ebf = [self.carve(64) for _ in range(NS)]
        Enbf = [self.carve(64) for _ in range(NS)]
        oacc = self.carve(T)
        rT = self.carve(T, F32R)
        rt = self.carve(512)
        rr = self.carve(512)
        stg = [self.carve(512) for _ in range(2)]
        gn = self.vecs[:, 0:1]
        for h in range(6):
            hk = ("h", h)
            self.dma("pool", qT[0:64, :], self.zT[256 + h * 64:256 + (h + 1) * 64, :], [], ["qT"])
            self.dma("pool", kT[0:64, :], self.zT[640 + h * 64:640 + (h + 1) * 64, :], [], ["kT"])
            self.dma("pool", vt[0:64, :, :], self.vtok[:, h * 128:(h + 1) * 128].rearrange("(n p) d -> p n d", p=64), [], ["vt"])
            self.dma("pool", kt[0:64, :, :], self.ktok[:, h * 64:(h + 1) * 64].rearrange("(n p) d -> p n d", p=64), [], ["kt"])
            self.dma("pool", rT, self.zT[1792 + h * 128:1792 + (h + 1) * 128, :], [], ["rT"])
            cnt = [0]

            def prep(st, d):
                n = order[d][st]
                sl = (st * 2 + d) % NS
                c0, c1 = n * 64, (n + 1) * 64
                k1 = ("sl", sl)
                pb, pk = self.bank("a")
                self.mm(pb[0:64, 0:64], zg[0:33, c0:c1], self.gw[0:33, d * 384 + h * 64:d * 384 + (h + 1) * 64], True, True,
                        ["zg", "zg1", "gw"], [pk])
                self.act(ef[sl][0:64, :], pb[0:64, 0:64], AF.Exp, [pk], [("ef", sl)], scale=-1.0)
                self.act(ef[sl][0:64, :], ef[sl][0:64, :], AF.Ln, [("ef", sl)], [("ef", sl)], bias=self.one_ap[0:64, :])
                self.dve(lambda e: e.tensor_scalar(out=logg[sl][0:64, :], in0=ef[sl][0:64, :], scalar1=-1.0 / 16.0,
                                                   scalar2=None, op0=ALU.mult), [("ef", sl)], [("logg", sl)])
                pD, pDk = self.bank("a")
                self.mm(pD[0:64, 0:64], SX[d], logg[sl][0:64, :], True, True, ["glacR", ("logg", sl)], [pDk])
                pB, pBk = self.bank("a")
                self.mm(pB[0:64, 0:64], logg[sl][0:64, :], TX[d], True, True, ["glacR", ("logg", sl)], [pBk])
                self.act(EDf[sl][0:64, :], pD[0:64, 0:64], AF.Exp, [pDk], [("ED", sl)])
                self.dve(lambda e: e.tensor_tensor(out=kdec[sl][0:64, :], in0=EDf[sl][0:64, :], in1=kt[0:64, n, :], op=ALU.mult),
                         [("ED", sl), "kt"], [("kdec", sl)])
                self.act(Ebf[sl][0:64, :], pB[0:64, 0:64], AF.Exp, [pBk], [("Eb", sl)])
                self.act(Enbf[sl][0:64, :], pB[0:64, 0:64], AF.Exp, [pBk], [("Enb", sl)], scale=-1.0)
                self.dve(lambda e: e.scalar_tensor_tensor(out=qd[sl][0:64, :], in0=qT[0:64, c0:c1], scalar=0.125,
                                                          in1=Ebf[sl][0:64, :], op0=ALU.mult, op1=ALU.mult),
                         ["qT", ("Eb", sl)], [("qd", sl)])
                self.dve(lambda e: e.tensor_tensor(out=kd[sl][0:64, :], in0=kT[0:64, c0:c1], in1=Enbf[sl][0:64, :], op=ALU.mult),
                         ["kT", ("Enb", sl)], [("kd", sl)])
                pA, pAk = self.bank("a")
                self.mm(pA[0:64, 0:64], kd[sl][0:64, :], qd[sl][0:64, :], True, True, [("kd", sl), ("qd", sl)], [pAk])
                self.dve(lambda e: e.tensor_tensor(out=attm[sl][0:64, :], in0=pA[0:64, 0:64], in1=MK[d], op=ALU.mult),
                         [pAk, "glac"], [("attm", sl)])

            def recur(st, d):
                n = order[d][st]
                sl = (st * 2 + d) % NS
                c0, c1 = n * 64, (n + 1) * 64
                cur, nxt = S[d][st % 2], S[d][(st + 1) % 2]
                pO, pOk = self.bank("b")
                self.mm(pO[:, 0:64], vt[0:64, n, :], attm[sl][0:64, :], True, st == 0, ["vt", ("attm", sl)], [pOk])
                if st > 0:
                    self.mm(pO[:, 0:64], cur[0:64, :], qd[sl][0:64, :], False, True, [("S", d, st % 2), ("qd", sl)], [pOk])
                other = step_of[1 - d][n]
                first_touch = (st, d) < (other, 1 - d)
                if first_touch:
                    self.act(oacc[:, c0:c1], pO[:, 0:64], AF.Copy, [pOk], [("oacc", n)])
                else:
                    self.dve(lambda e: e.tensor_tensor(out=oacc[:, c0:c1], in0=pO[:, 0:64], in1=oacc[:, c0:c1], op=ALU.add),
                             [pOk, ("oacc", n)], [("oacc", n)])
                pS, pSk = self.bank("b")
                self.mm(pS[0:64, 0:128], kdec[sl][0:64, :], vt[0:64, n, :], True, True, [("kdec", sl), "vt"], [pSk])
                if st == 0:
                    self.dve(lambda e: e.tensor_copy(out=nxt[0:64, :], in_=pS[0:64, 0:128]), [pSk], [("S", d, (st + 1) % 2)])
                else:
                    col = 63 if d == 0 else 0
                    self.dve(lambda e: e.scalar_tensor_tensor(out=nxt[0:64, :], in0=cur[0:64, :], scalar=Ebf[sl][0:64, col:col + 1],
                                                              in1=pS[0:64, 0:128], op0=ALU.mult, op1=ALU.add),
                             [("S", d, st % 2), ("Eb", sl), pSk], [("S", d, (st + 1) % 2)])

            for st in range(NCH + LOOK):
                if st < NCH:
                    prep(st, 0)
                    prep(st, 1)
                if st >= LOOK:
                    recur(st - LOOK, 0)
                    recur(st - LOOK, 1)
            self.act(rT, rT, AF.Silu, ["rT"], ["rT"])
            for bi, (t0, tn) in enumerate(blocks_of(0, T, 512)):
                s = bi % 2
                okeys = [("oacc", n) for n in range(t0 // 64, (t0 + tn) // 64)]
                self.act(sqr[s][:, :tn], oacc[:, t0:t0 + tn], AF.Square, okeys, [("sqr", s)])
                pb, pk = self.bank("a")
                self.mm(pb[:, :tn], self.ones_r, sqr[s][:, :tn], True, True, ["ones", ("sqr", s)], [pk])
                self.act(rt[:, :tn], pb[:, :tn], AF.Sqrt, [pk], ["rt"], bias=self.eps_ap, scale=1.0 / 128.0)
                self.dve(lambda e, tn=tn: e.reciprocal(out=rr[:, :tn], in_=rt[:, :tn]), ["rt"], ["rr"])
                self.dve(lambda e, t0=t0, tn=tn, s=s: e.scalar_tensor_tensor(out=stg[s][:, :tn], in0=oacc[:, t0:t0 + tn], scalar=gn,
                                                                        in1=rr[:, :tn], op0=ALU.mult, op1=ALU.mult),
                         okeys + ["rr", "vecs"], [("stg", s)])
                self.dve(lambda e, t0=t0, tn=tn, s=s: e.tensor_tensor(out=stg[s][:, :tn], in0=stg[s][:, :tn], in1=rT[:, t0:t0 + tn],
                                                                 op=ALU.mult), [("stg", s), "rT"], [("stg", s)])
                self.dma("sp", self.mixT[256 + h * 128:256 + (h + 1) * 128, t0:t0 + tn], stg[s][:, :tn], [("stg", s)],
                         [("mixT", h, bi)])
        self.P.barrier()

    def phase_outproj(self, layer, t_end):
        self.reset()
        hT, mG = self.hT, self.mG
        w = self.carve(8 * D, F32R).rearrange("p (k n) -> p k n", k=8)
        self.dma("pool", w, self.I["mix_w_out"][layer].rearrange("(k p) n -> p k n", p=128), [], ["wmo"])
        mb = [self.carve(8 * 512, F32R).rearrange("p (k t) -> p k t", k=8) for _ in range(2)]
        for bi, (t0, tn) in enumerate(blocks_of(0, t_end)):
            s = bi % 2
            self.dma("pool", mb[s][:, :, :tn], self.mixT.rearrange("(k p) t -> p k t", p=128)[:, :, t0:t0 + tn], [], [("mb", s)])
            for f in range(8):
                pb, pk = self.bank("all")
                for k in range(8):
                    self.mm(pb[:, :tn], w[:, k, f * 128:(f + 1) * 128], mb[s][:, k, :tn], k == 0, k == 7, ["wmo", ("mb", s)], [pk])
                for (c, off, a, n) in seg_cols(t0, tn):
                    self.dve(lambda e, pb=pb, f=f, a=a, n=n, c=c, off=off: e.scalar_tensor_tensor(
                        out=hT[:, f, a:a + n], in0=pb[:, off:off + n], scalar=mG[:, 1, f, c:c + 1],
                        in1=hT[:, f, a:a + n], op0=ALU.mult, op1=ALU.add),
                        [pk, ("mG", 1, c), ("h", f, bi)], [("h", f, bi)])
        self.P.barrier()

    def phase_conv(self):
        self.reset()
        cv = self.convv
        bufs = [self.carve(TL) for _ in range(3)]
        for c in range(2):
            zb, zc, zx = bufs
            for j, b in enumerate(bufs):
                self.dma("sp", b, self.zT[j * 256 + c * 128:j * 256 + (c + 1) * 128, 0:TL], [], [("cb", j)])
            self.dve(lambda e: e.tensor_tensor(out=zc, in0=zc, in1=zx, op=ALU.mult), [("cb", 1), ("cb", 2)], [("cb", 1)])
            self.dve(lambda e, c=c: e.tensor_scalar(out=zx, in0=zc, scalar1=cv[:, c * 4 + 1:c * 4 + 2], scalar2=cv[:, c * 4 + 3:c * 4 + 4],
                                               op0=ALU.mult, op1=ALU.add), [("cb", 1), "convv"], [("cb", 2)])
            self.dve(lambda e, c=c: e.scalar_tensor_tensor(out=zx[:, 1:TL], in0=zc[:, 0:TL - 1], scalar=cv[:, c * 4:c * 4 + 1],
                                                      in1=zx[:, 1:TL], op0=ALU.mult, op1=ALU.add), [("cb", 1), ("cb", 2)], [("cb", 2)])
            self.dve(lambda e, c=c: e.scalar_tensor_tensor(out=zx[:, 0:TL - 1], in0=zc[:, 1:TL], scalar=cv[:, c * 4 + 2:c * 4 + 3],
                                                      in1=zx[:, 0:TL - 1], op0=ALU.mult, op1=ALU.add), [("cb", 1), ("cb", 2)], [("cb", 2)])
            self.dve(lambda e: e.tensor_tensor(out=zb, in0=zb, in1=zx, op=ALU.mult), [("cb", 0), ("cb", 2)], [("cb", 0)])
            self.dma("sp", self.mixT[c * 128:(c + 1) * 128, 0:TL], zb, [("cb", 0)], [("mixc", c)])
        self.P.barrier()

    def phase_dattn(self):
        I = self.I
        self.reset()
        cosT = self.carve(TL)
        sinT = self.carve(TL)
        fb = [self.carve(512) for _ in range(6)]
        qT = self.carve(TL, F32R)
        qz = self.carve(2 * TL, F32R).rearrange("p (s t) -> p s t", s=2)
        kT = self.carve(T, F32R)
        vt = self.carve(T, F32R).rearrange("p (n d) -> p n d", d=128)
        NE = 4
        E = [self.carve(512, F32R) for _ in range(NE)]
        sqr = self.carve(512, F32R)
        self.dma("sp", cosT, I["rope"][0], [], ["cosT"])
        self.dma("sp", sinT, I["rope"][1], [], ["sinT"])
        self.pools["s4"] = [0, 1, 2, 3]
        ne = 0
        self.dve(lambda e: e.tensor_scalar(out=qz[64:128, 0, :], in0=cosT[64:128, :], scalar1=0.0, scalar2=None, op0=ALU.mult),
                 ["cosT"], ["qz0"])
        self.dve(lambda e: e.tensor_scalar(out=qz[0:64, 1, :], in0=cosT[0:64, :], scalar1=0.0, scalar2=None, op0=ALU.mult),
                 ["cosT"], ["qz1"])
        for h in range(6):
            self.dma("pool", qT, self.zT[768 + h * 128:768 + (h + 1) * 128, 0:TL], [], ["qT"])
            self.dma("pool", kT, self.zT[1536 + h * 128:1536 + (h + 1) * 128, :], [], ["kT"])
            self.dma("pool", vt, self.vtok[:, h * 128:(h + 1) * 128].rearrange("(n p) d -> p n d", p=128), [], ["vt"])
            for (x, xk) in ((qT, "qT"), (kT, "kT")):
                for qb in range(4):
                    c0, c1 = qb * 512, (qb + 1) * 512
                    pb, pk = self.bank("s4")
                    self.mm(pb[:, :], self.permT, x[:, c0:c1], True, True, ["permT", xk], [pk])
                    t1, t2 = fb[(qb % 2) * 2], fb[(qb % 2) * 2 + 1]
                    k1, k2 = ("fb", (qb % 2) * 2), ("fb", (qb % 2) * 2 + 1)
                    self.dve(lambda e, t1=t1, pb=pb, c0=c0, c1=c1: e.tensor_tensor(out=t1, in0=pb[:, :], in1=sinT[:, c0:c1], op=ALU.mult),
                             [pk, "sinT"], [k1])
                    self.dve(lambda e, t2=t2, x=x, c0=c0, c1=c1: e.tensor_tensor(out=t2, in0=x[:, c0:c1], in1=cosT[:, c0:c1], op=ALU.mult),
                             [xk, "cosT"], [k2])
                    if xk == "kT":
                        self.dve(lambda e, t1=t1, t2=t2, x=x, c0=c0, c1=c1: e.tensor_tensor(out=x[:, c0:c1], in0=t1, in1=t2, op=ALU.add),
                                 [k1, k2], [xk])
                    else:
                        self.dve(lambda e, t1=t1, t2=t2, c0=c0, c1=c1: e.tensor_tensor(out=qz[0:64, 0, c0:c1], in0=t1[0:64, :],
                                                                                   in1=t2[0:64, :], op=ALU.add), [k1, k2], ["qzr0"])
                        self.dve(lambda e, t1=t1, t2=t2, c0=c0, c1=c1: e.tensor_tensor(out=qz[64:128, 1, c0:c1], in0=t1[64:128, :],
                                                                                   in1=t2[64:128, :], op=ALU.add), [k1, k2], ["qzr1"])
            for qb in range(4):
                c0, c1 = qb * 512, (qb + 1) * 512
                acc = {}
                for sub in range(2):
                    acc[("Z", sub)] = (self.ps[4 + sub * 2], ("ps", 4 + sub * 2))
                    acc[("O", sub)] = (self.ps[5 + sub * 2], ("ps", 5 + sub * 2))
                its = [(kt_, sub) for kt_ in range(T // 128) for sub in range(2)]
                slots = {}

                def score(ii):
                    nonlocal ne
                    kt_, sub = its[ii]
                    pS, pSk = self.bank("s4")
                    self.mm(pS[:, :], kT[:, kt_ * 128:(kt_ + 1) * 128], qz[:, sub, c0:c1], True, True,
                            ["kT", "qz0", "qz1", "qzr0", "qzr1"], [pSk])
                    es = ne % NE
                    ne += 1
                    self.act(E[es], pS[:, :], AF.Exp, [pSk], [("E", es)], scale=0.125)
                    slots[ii] = es

                score(0)
                score(1)
                for ii, (kt_, sub) in enumerate(its):
                    if ii + 2 < len(its):
                        score(ii + 2)
                    es = slots[ii]
                    zb_, zk = acc[("Z", sub)]
                    ob_, ok = acc[("O", sub)]
                    self.mm(zb_[:, :], self.ones_r, E[es], kt_ == 0, kt_ == T // 128 - 1, ["ones", ("E", es)], [zk])
                    self.mm(ob_[:, :], vt[:, kt_, :], E[es], kt_ == 0, kt_ == T // 128 - 1, ["vt", ("E", es)], [ok])
                b0, b1, b2, b3 = fb[0], fb[1], fb[2], fb[3]
                bo = fb[4 + qb % 2]
                kk = [("fb", i) for i in range(4)]
                ko = ("fb", 4 + qb % 2)
                Z0, O0, Z1, O1 = acc[("Z", 0)], acc[("O", 0)], acc[("Z", 1)], acc[("O", 1)]
                self.act(b0, Z0[0][:, :], AF.Copy, [Z0[1]], [kk[0]])
                self.act(b1, Z1[0][:, :], AF.Copy, [Z1[1]], [kk[1]])
                self.dve(lambda e, O0=O0: e.tensor_copy(out=b2, in_=O0[0][:, :]), [O0[1]], [kk[2]])
                self.dve(lambda e, O1=O1: e.tensor_copy(out=b3, in_=O1[0][:, :]), [O1[1]], [kk[3]])
                self.dve(lambda e: e.reciprocal(out=b0, in_=b0), [kk[0]], [kk[0]])
                self.dve(lambda e: e.reciprocal(out=b1, in_=b1), [kk[1]], [kk[1]])
                self.dve(lambda e: e.tensor_tensor(out=b2, in0=b2, in1=b0, op=ALU.mult), [kk[2], kk[0]], [kk[2]])
                self.dve(lambda e: e.tensor_tensor(out=b3, in0=b3, in1=b1, op=ALU.mult), [kk[3], kk[1]], [kk[3]])
                neglam = self.neglam
                self.dve(lambda e: e.scalar_tensor_tensor(out=b2, in0=b3, scalar=neglam, in1=b2, op0=ALU.mult, op1=ALU.add),
                         [kk[3], kk[2], "neglam"], [kk[2]])
                self.act(sqr, b2, AF.Square, [kk[2]], ["sqr"])
                pb, pk = self.bank("s4")
                self.mm(pb[:, :], self.ones_r, sqr, True, True, ["ones", "sqr"], [pk])
                self.act(b0, pb[:, :], AF.Sqrt, [pk], [kk[0]], bias=self.eps_ap, scale=1.0 / 128.0)
                self.dve(lambda e: e.reciprocal(out=b1, in_=b0), [kk[0]], [kk[1]])
                dnl = self.vecs[:, 2:3]
                self.dve(lambda e, bo=bo: e.scalar_tensor_tensor(out=bo, in0=b2, scalar=dnl, in1=b1, op0=ALU.mult, op1=ALU.mult),
                         [kk[2], kk[1], "vecs2"], [ko])
                b3 = bo
                kk = kk[:3] + [ko]
                self.dma("sp", self.mixT[256 + h * 128:256 + (h + 1) * 128, c0:c1], b3, [kk[3]], [("mixo", h, qb)])
        self.P.barrier()

    def phase_final(self):
        self.reset()
        hT = self.hT
        gbc = self.carve(D)
        self.dma("sp", gbc, self.I["fnorm_bc"], [], ["gbc"])
        xt = [self.carve(D) for _ in range(2)]
        yt = [self.carve(D) for _ in range(2)]
        junk = self.carve(D)
        ss = [self.carve(1) for _ in range(2)]
        rs = [self.carve(1) for _ in range(2)]
        outs = []
        for tt in range(TL // 128):
            s = tt % 2
            for g in range(2):
                pb, pk = self.bank("all")
                for kk in range(4):
                    k = g * 4 + kk
                    o = pb[:, kk * 128:(kk + 1) * 128]
                    i_ = hT[:, k, tt * 128:(tt + 1) * 128]
                    idn = self.ident
                    self.P.add("pe", lambda e, o=o, i_=i_, idn=idn: e.transpose(o, i_, idn), reads=["ident"], writes=[pk])
                self.evac(g, xt[s][:, g * 512:(g + 1) * 512], pb[:, :], [pk], [("xt", s, g)])
            xs, ys, sss, rss = xt[s], yt[s], ss[s], rs[s]
            self.P.add("act", lambda e, xs=xs, sss=sss: e.activation(out=junk, in_=xs, func=AF.Square, accum_out=sss),
                       reads=[("xt", s, 0), ("xt", s, 1)], writes=[("ss", s), "junk"])
            self.act(rss, sss, AF.Sqrt, [("ss", s)], [("rs0", s)], bias=self.eps_ap, scale=1.0 / D)
            self.dve(lambda e, rss=rss: e.reciprocal(out=rss, in_=rss), [("rs0", s)], [("rs", s)])
            self.dve(lambda e, xs=xs, ys=ys, rss=rss: e.scalar_tensor_tensor(out=ys, in0=xs, scalar=rss, in1=gbc, op0=ALU.mult,
                                                                         op1=ALU.mult),
                     [("xt", s, 0), ("xt", s, 1), ("rs", s), "gbc"], [("yt", s)])
            self.dma("sp", self.out[tt * 128:(tt + 1) * 128, :], ys, [("yt", s)], [("out", tt)])
            outs.append(("out", tt))
        self.P.add("sp", None, reads=outs)

    def phase_dump(self):
        self.dma("sp", self.dbg, self.hT.rearrange("p k t -> p (k t)"), [], ["dbg"])
        self.P.add("sp", None, reads=["dbg"])


def fm(v):
    v = np.asarray(v, np.float32)
    lead = v.shape[:-1]
    r = v.reshape(lead + (v.shape[-1] // 128, 128))
    return np.ascontiguousarray(np.moveaxis(r, -1, 0))


def prep_inputs(inp, b):
    m = {}
    m["x"] = np.ascontiguousarray(inp["x"][b])
    m["ctx"] = np.ascontiguousarray(inp["ctx"][b])
    cond = np.stack([fm(inp["c"][b]), fm(inp["c_ctx"])], axis=-1)
    m["cond"] = np.ascontiguousarray(cond.reshape(128, 16))
    m["ada_w"] = inp["ada_w"]
    m["ada_b"] = np.ascontiguousarray(fm(inp["ada_b"]).reshape(128, 144))
    g = np.stack([inp["norm_ffn1"][0], inp["norm_ffn1"][1], inp["norm_mix"][0], inp["norm_mix"][1],
                  inp["norm_ffn2"][0], inp["norm_ffn2"][1], inp["final_norm"]], axis=0)
    m["gains"] = np.ascontiguousarray(fm(g).reshape(128, 56))
    for k in ("ffn1_w_in", "ffn1_w_out", "ffn2_w_in", "ffn2_w_out"):
        m[k] = inp[k]
    m["ident"] = np.eye(128, dtype=np.float32)
    m["mix_w_out"] = inp["mix_w_out"]
    m["even_w_in"] = np.ascontiguousarray(inp["even_w_in"][0])
    m.update(CONSTS())
    gw = np.zeros((33, 768), np.float32)
    gw[0:16, 0:384] = inp["gla_gate_w"][0, 0]
    gw[32, 0:384] = inp["gla_gate_b"][0, 0]
    gw[16:32, 384:768] = inp["gla_gate_w"][0, 1]
    gw[32, 384:768] = inp["gla_gate_b"][0, 1]
    m["gw_pad"] = gw
    v = np.zeros((128, 8), np.float32)
    v[:, 0] = inp["gla_norm"][0]
    v[:, 1] = inp["diff_norm"][0]
    m["vecs128"] = v
    m["odd_w_in"] = np.ascontiguousarray(inp["odd_w_in"][0])
    m["lamv"] = np.ascontiguousarray(np.stack([inp["lambda_q1"][0], inp["lambda_k1"][0], inp["lambda_q2"][0],
                                               inp["lambda_k2"][0]], axis=1).astype(np.float32))
    cv = np.zeros((128, 8), np.float32)
    for c in range(2):
        for j in range(3):
            cv[:, c * 4 + j] = inp["conv_w"][0, j, c * 128:(c + 1) * 128]
        cv[:, c * 4 + 3] = inp["conv_b"][0, c * 128:(c + 1) * 128]
    m["convv"] = cv
    m["fnorm_bc"] = np.ascontiguousarray(np.broadcast_to(inp["final_norm"].astype(np.float32)[None, :], (128, D)))
    return m


_CONSTS = None


def CONSTS():
    global _CONSTS
    if _CONSTS is not None:
        return _CONSTS
    c = {}

    def dft(n, scale):
        i = np.arange(n, dtype=np.int64)
        ang = 2.0 * np.pi * ((i[:, None] * i[None, :]) % n).astype(np.float64) / n
        return (np.cos(ang) * scale).astype(np.float32), (-np.sin(ang) * scale).astype(np.float32)

    cl, sl = dft(TL, 1.0 / math.sqrt(TL * 64.0))
    c["dft_lat"] = np.stack([cl, sl])
    cc, sc_ = dft(TC, 1.0 / math.sqrt(TC * 64.0))
    c["dft_ctx"] = np.stack([cc, sc_])
    i = np.arange(64, dtype=np.int64)
    a64 = 2.0 * np.pi * ((i[:, None] * i[None, :]) % 64).astype(np.float64) / 64
    bd = np.zeros((128, 256), np.float32)
    for g in range(2):
        bd[g * 64:(g + 1) * 64, g * 64:(g + 1) * 64] = np.cos(a64)
        bd[g * 64:(g + 1) * 64, 128 + g * 64:128 + (g + 1) * 64] = np.sin(a64)
    c["bd64"] = bd
    tp, t = i[:, None], i[None, :]
    g = np.zeros((64, 6 * 64), np.float32)
    g[:, 0:64] = (tp > t)
    g[:, 64:128] = (tp < t)
    g[:, 128:192] = (tp <= t)
    g[:, 192:256] = (tp >= t)
    g[:, 256:320] = (t >= tp)
    g[:, 320:384] = (t <= tp)
    c["gla_c"] = g
    c["ones_row"] = np.ones((1, T), np.float32)
    inv = (np.float32(10000.0) ** (-(np.arange(16, dtype=np.float32)) / np.float32(16))).astype(np.float32)
    tpos = np.arange(TL)
    row = (tpos // 64).astype(np.float32)
    colp = (tpos % 64).astype(np.float32)
    rope = np.zeros((2, 128, TL), np.float32)
    perm = np.zeros((128, 128), np.float32)
    for p in range(128):
        d = p % 64
        axis, half, fi = d // 32, (d % 32) // 16, d % 16
        ang = ((row if axis == 0 else colp) * inv[fi]).astype(np.float32).astype(np.float64)
        rope[0, p] = np.cos(ang)
        rope[1, p] = (-np.sin(ang)) if half == 0 else np.sin(ang)
        partner = p + 16 if half == 0 else p - 16
        perm[partner, p] = 1.0
    c["rope"] = rope
    c["permT"] = perm
    _CONSTS = c
    return c


def kernel(**inp):
    inp = {k: np.asarray(v) for k, v in inp.items()}
    bld = Builder()
    nc = bld.build()
    in_maps = [prep_inputs(inp, b) for b in range(8)]
    res = run_bass_kernel_spmd(nc, in_maps, core_ids=list(range(8)))
    return np.stack([r["out"] for r in res.results], axis=0)
```

```python
import math
from contextlib import ExitStack
import numpy as np
import concourse.bass as bass
import concourse.mybir as mybir
from concourse.bass_utils import run_bass_kernel_spmd

F32 = mybir.dt.float32
F32R = mybir.dt.float32r
AF = mybir.ActivationFunctionType
ALU = mybir.AluOpType

D = 1024
TL = 2048
TC = 256
T = TL + TC
HID = 2816
NJ = HID // 128
EPS = 1e-6
LAM_INIT = 0.8 - 0.6 * math.exp(-0.3 * 1)
NF = 27600
NR = 25600


class Op:
    __slots__ = ("eng", "fn", "dma", "deps", "needs_inc", "ticket", "idx")


class Prog:
    ENGS = ("pe", "act", "dve", "pool", "sp")

    def __init__(self):
        self.ops = []
        self.last_w = {}
        self.readers = {}

    def add(self, eng, fn, reads=(), writes=(), dma=False, barrier=False):
        op = Op()
        op.eng, op.fn, op.dma, op.needs_inc, op.ticket = eng, fn, dma, False, None
        op.idx = len(self.ops)
        hard, war = set(), set()
        for k in reads:
            w = self.last_w.get(k)
            if w is not None:
                hard.add(w)
        for k in writes:
            w = self.last_w.get(k)
            if w is not None:
                hard.add(w)
            for r in self.readers.get(k, ()):
                war.add(r)
        deps = []
        for d in hard | war:
            if d is op:
                continue
            if not d.dma and not dma and d.eng == eng and not barrier:
                if eng == "pe":
                    continue
                if d not in hard:
                    continue
            deps.append(d)
        for d in deps:
            d.needs_inc = True
        op.deps = deps
        for k in reads:
            self.readers.setdefault(k, []).append(op)
        for k in writes:
            self.last_w[k] = op
            self.readers[k] = []
        self.ops.append(op)
        return op

    def barrier(self):
        allk = list(self.last_w.keys() | self.readers.keys())
        for e in self.ENGS:
            self.add(e, None, writes=allk + ["__barrier__"], barrier=True)
        self.last_w = {}
        self.readers = {}

    def emit(self, nc, block, sems, dma_sems):
        cnt = {e: 0 for e in self.ENGS}
        dcnt = {}
        dnext = {e: 0 for e in self.ENGS}
        dprev = {}
        for op in self.ops:
            if op.dma:
                pool = dma_sems[op.eng]
                s = pool[dnext[op.eng] % len(pool)]
                dnext[op.eng] += 1
                prev = dcnt.get(id(s), 0)
                dprev[op.idx] = (s, prev)
                dcnt[id(s)] = prev + 16
                op.ticket = (s, prev + 16)
            elif op.needs_inc:
                cnt[op.eng] += 1
                op.ticket = (sems[op.eng], cnt[op.eng])
        per = {e: [o for o in self.ops if o.eng == e] for e in self.ENGS}

        def body(eng_name):
            def run(e):
                known = {}
                for op in per[eng_name]:
                    waits = {}
                    for d in op.deps:
                        s, v = d.ticket
                        if known.get(id(s), 0) < v and waits.get(id(s), (None, 0))[1] < v:
                            waits[id(s)] = (s, v)
                    if op.dma:
                        s, v = dprev[op.idx]
                        if v > 0 and known.get(id(s), 0) < v and waits.get(id(s), (None, 0))[1] < v:
                            waits[id(s)] = (s, v)
                    for s, v in waits.values():
                        e.wait_ge(s, v)
                        known[id(s)] = v
                    if op.fn is None:
                        if op.needs_inc:
                            e.nop().then_inc(op.ticket[0], 1)
                        continue
                    ins = op.fn(e)
                    if op.dma:
                        ins.then_inc(op.ticket[0], 16)
                    elif op.needs_inc:
                        ins.then_inc(op.ticket[0], 1)
            return run

        block.tensor(body("pe"))
        block.scalar(body("act"))
        block.vector(body("dve"))
        block.gpsimd(body("pool"))
        block.sync(body("sp"))


def seg_cols(t0, tn):
    out = []
    a, b = t0, min(t0 + tn, TL)
    if b > a:
        out.append((0, 0, a, b - a))
    a2, b2 = max(t0, TL), t0 + tn
    if b2 > a2:
        out.append((1, a2 - t0, a2, b2 - a2))
    return out


def halves_of(t_end):
    return [(0, 1024), (1024, t_end)]


HMAX = 1280


def blocks_of(t0, t1, bs=512):
    out = []
    t = t0
    while t < t1:
        n = min(bs, t1 - t)
        out.append((t, n))
        t += n
    return out


class Builder:
    def __init__(self, stop_after=None):
        self.stop_after = stop_after
        self.nc = bass.Bass("TRN2", target_bir_lowering=False)
        self.P = Prog()
        self.es = ExitStack()
        self.bank_i = 0
        self.uid = 0

    def dram_in(self, name, shape):
        return self.nc.dram_tensor(name, list(shape), F32, kind="ExternalInput").ap()

    def carve(self, n, dtype=F32):
        if dtype is F32R:
            assert self.topr + n <= NR, (self.topr, n)
            ap = self.arenaR[:, self.topr:self.topr + n]
            self.topr += n
            return ap
        assert self.top + n <= NF, (self.top, n)
        ap = self.arena[:, self.top:self.top + n]
        self.top += n
        return ap

    def reset(self):
        self.top = self.base_top
        self.topr = self.base_topr

    def bank(self, pool):
        lst = self.pools[pool]
        i = self.pool_i.get(pool, 0)
        self.pool_i[pool] = i + 1
        b = lst[i % len(lst)]
        return self.ps[b], ("ps", b)

    def mm(self, out, lhsT, rhs, start, stop, reads, writes):
        self.P.add("pe", lambda e: e.matmul(out, lhsT, rhs, start=start, stop=stop), reads=reads, writes=writes)

    def act(self, out, in_, func, reads, writes, bias=None, scale=None):
        kw = {}
        if bias is not None:
            kw["bias"] = bias
        if scale is not None:
            kw["scale"] = scale
        self.P.add("act", lambda e: e.activation(out=out, in_=in_, func=func, **kw), reads=reads, writes=writes)

    def dve(self, fn, reads, writes):
        self.P.add("dve", fn, reads=reads, writes=writes)

    def dma(self, q, out, in_, reads, writes):
        self.P.add(q, lambda e: e.dma_start(out=out, in_=in_), reads=reads, writes=writes, dma=True)

    def build(self):
        nc, P, es = self.nc, self.P, self.es
        I = {}
        I["x"] = self.dram_in("x", (TL, D))
        I["ctx"] = self.dram_in("ctx", (TC, D))
        I["cond"] = self.dram_in("cond", (128, 16))
        I["ada_w"] = self.dram_in("ada_w", (2, D, 9 * D))
        I["ada_b"] = self.dram_in("ada_b", (128, 2 * 72))
        I["gains"] = self.dram_in("gains", (128, 7 * 8))
        I["ffn1_w_in"] = self.dram_in("ffn1_w_in", (2, D, 2 * HID))
        I["ffn1_w_out"] = self.dram_in("ffn1_w_out", (2, HID, D))
        I["ffn2_w_in"] = self.dram_in("ffn2_w_in", (2, D, 2 * HID))
        I["ffn2_w_out"] = self.dram_in("ffn2_w_out", (2, HID, D))
        I["ident"] = self.dram_in("ident", (128, 128))
        I["mix_w_out"] = self.dram_in("mix_w_out", (2, D, D))
        I["even_w_in"] = self.dram_in("even_w_in", (D, 2592))
        I["dft_lat"] = self.dram_in("dft_lat", (2, TL, TL))
        I["dft_ctx"] = self.dram_in("dft_ctx", (2, TC, TC))
        I["bd64"] = self.dram_in("bd64", (128, 256))
        I["gla_c"] = self.dram_in("gla_c", (64, 6 * 64))
        I["gw_pad"] = self.dram_in("gw_pad", (33, 768))
        I["ones_row"] = self.dram_in("ones_row", (1, T))
        I["vecs128"] = self.dram_in("vecs128", (128, 8))
        I["odd_w_in"] = self.dram_in("odd_w_in", (D, 3072))
        I["rope"] = self.dram_in("rope", (2, 128, TL))
        I["permT"] = self.dram_in("permT", (128, 128))
        I["lamv"] = self.dram_in("lamv", (64, 4))
        I["convv"] = self.dram_in("convv", (128, 8))
        I["fnorm_bc"] = self.dram_in("fnorm_bc", (128, D))
        self.I = I
        skind = "ExternalOutput" if self.stop_after is not None else "Internal"
        self.zT = nc.dram_tensor("zT", [3072, T], F32, kind=skind).ap()
        self.vtok = nc.dram_tensor("vtok", [T, 768], F32, kind=skind).ap()
        self.ktok = nc.dram_tensor("ktok", [T, 384], F32, kind=skind).ap()
        self.mixT = nc.dram_tensor("mixT", [D, T], F32, kind=skind).ap()
        if self.stop_after is not None:
            self.dbg = nc.dram_tensor("dbg", [128, 8 * T], F32, kind="ExternalOutput").ap()
        else:
            self.out = nc.dram_tensor("out", [TL, D], F32, kind="ExternalOutput").ap()

        self.arena = es.enter_context(nc.sbuf_tensor("arena", [128, NF], F32))
        self.arenaR = es.enter_context(nc.sbuf_tensor("arenaR", [128, NR], F32R))
        self.ps = [es.enter_context(nc.psum_tensor(f"ps{i}", [128, 512], F32)) for i in range(8)]
        sems = {e: es.enter_context(nc.semaphore(f"s_{e}")) for e in Prog.ENGS}
        dma_sems = {e: [es.enter_context(nc.semaphore(f"d_{e}{i}")) for i in range(8)] for e in ("sp", "pool", "act")}
        dma_sems["pe"] = dma_sems["dve"] = []
        self.pools = {"a": [0, 1, 2, 3], "b": [4, 5, 6, 7], "all": list(range(8))}
        self.pool_i = {}

        self.top = 0
        self.topr = 0
        self.hT = self.carve(8 * T).rearrange("p (k t) -> p k t", k=8)
        self.ident = self.carve(128)
        self.ones_r = self.carve(128, F32R)
        self.ones_f = self.carve(128)
        self.cond = self.carve(16).rearrange("p (k c) -> p k c", c=2)
        self.sc = self.carve(16, F32R).rearrange("p (k c) -> p k c", c=2)
        self.adab = self.carve(144).rearrange("p (l j) -> p l j", l=2)
        self.gains = self.carve(56).rearrange("p (g k) -> p g k", k=8)
        self.mod = self.carve(144).rearrange("p (j c) -> p j c", c=2)
        self.mA = self.carve(48).rearrange("p (i k c) -> p i k c", i=3, k=8)
        self.mG = self.carve(48).rearrange("p (i k c) -> p i k c", i=3, k=8)
        self.eps_ap = self.carve(1)
        self.one_ap = self.carve(1)
        self.vecs = self.carve(8)
        self.glac = self.carve(6 * 64)
        self.glacR = self.carve(6 * 64, F32R)
        self.bd64 = self.carve(256, F32R)
        self.gw = self.carve(768, F32R)
        self.permT = self.carve(128, F32R)
        self.convv = self.carve(8)
        self.lamv = self.carve(4)
        self.lamp = self.carve(2, F32R)
        self.lame = self.carve(2)
        self.neglam = self.carve(1)
        self.base_top = self.top
        self.base_topr = self.topr

        self.phase_setup()
        self.phase_load()
        for layer in range(2):
            self.phase_mod(layer)
            self.phase_ffn(layer, 0, T)
            if self.stop_after == f"ffn1_{layer}":
                break
            if layer == 0:
                fm = [(o * 128, 128, o * 128) for o in list(range(0, 8)) + list(range(14, 20))] + [(2560, 32, 2560)]
                tok = [(1024, 512, self.vtok[:, 0:512]), (1536, 256, self.vtok[:, 512:768]), (640, 384, self.ktok)]
                self.phase_proj(layer, self.I["even_w_in"], fm, tok)
                if self.stop_after == "proj_0":
                    break
                self.phase_fourier()
                if self.stop_after == "fourier":
                    break
                self.phase_gla()
                if self.stop_after == "gla":
                    break
            else:
                fm = [(o * 128, 128, o * 128) for o in range(18)]
                tok = [(2304, 512, self.vtok[:, 0:512]), (2816, 256, self.vtok[:, 512:768])]
                self.phase_proj(layer, self.I["odd_w_in"], fm, tok)
                if self.stop_after == "proj_1":
                    break
                self.phase_conv()
                self.phase_dattn()
                if self.stop_after == "dattn":
                    break
            self.phase_outproj(layer, T if layer == 0 else TL)
            if self.stop_after == f"mix_{layer}":
                break
            self.phase_ffn(layer, 2, T if layer == 0 else TL)
            if self.stop_after == f"ffn2_{layer}":
                break
        if self.stop_after is not None:
            self.phase_dump()
        else:
            self.phase_final()

        with nc.Block() as block:
            P.emit(nc, block, sems, dma_sems)
        return nc

    def phase_setup(self):
        I = self.I
        self.dma("sp", self.ident, I["ident"], [], ["ident"])
        self.dma("sp", self.cond.rearrange("p k c -> p (k c)"), I["cond"], [], ["cond"])
        self.dma("sp", self.adab.rearrange("p l j -> p (l j)"), I["ada_b"], [], ["adab"])
        self.dma("sp", self.gains.rearrange("p g k -> p (g k)"), I["gains"], [], ["gains"])
        ones_r = self.ones_r
        ones_f = self.ones_f
        self.dve(lambda e: e.memset(ones_f, 1.0), [], ["ones_f"])
        self.dve(lambda e: e.tensor_copy(out=ones_r, in_=ones_f), ["ones_f"], ["ones"])
        eps_ap = self.eps_ap
        self.dve(lambda e: e.memset(eps_ap, EPS), [], ["eps"])
        one_ap = self.one_ap
        self.dve(lambda e: e.memset(one_ap, 1.0), [], ["one"])
        self.dma("sp", self.vecs, I["vecs128"], [], ["vecs"])
        self.dma("sp", self.glac[0:64, :], I["gla_c"], [], ["glac"])
        self.dma("pool", self.glacR[0:64, :], I["gla_c"], [], ["glacR"])
        self.dma("pool", self.bd64, I["bd64"], [], ["bd64"])
        self.dma("pool", self.gw[0:33, :], I["gw_pad"], [], ["gw"])
        self.dma("pool", self.permT, I["permT"], [], ["permT"])
        self.dma("sp", self.convv, I["convv"], [], ["convv"])
        self.dma("sp", self.lamv[0:64, :], I["lamv"], [], ["lamv"])
        lamv, lamp, lame, neglam, vecs = self.lamv, self.lamp, self.lame, self.neglam, self.vecs
        self.dve(lambda e: e.tensor_tensor(out=lamp[0:64, :], in0=lamv[0:64, 0:4:2], in1=lamv[0:64, 1:4:2], op=ALU.mult),
                 ["lamv"], ["lamp"])
        pb, pk = self.bank("all")
        self.mm(pb[:, 0:2], self.ones_r[0:64, :], lamp[0:64, :], True, True, ["ones", "lamp"], [pk])
        self.act(lame, pb[:, 0:2], AF.Exp, [pk], ["lame"])
        self.dve(lambda e: e.tensor_tensor(out=neglam, in0=lame[:, 1:2], in1=lame[:, 0:1], op=ALU.subtract), ["lame"], ["neglam0"])
        self.dve(lambda e: e.tensor_scalar(out=neglam, in0=neglam, scalar1=-LAM_INIT, scalar2=None, op0=ALU.add),
                 ["neglam0"], ["neglam"])
        self.dve(lambda e: e.tensor_scalar(out=vecs[:, 2:3], in0=vecs[:, 1:2], scalar1=1.0 - LAM_INIT, scalar2=None, op0=ALU.mult),
                 ["vecs"], ["vecs2"])
        self.act(self.sc, self.cond, AF.Silu, ["cond"], ["sc"])

    def phase_load(self):
        self.reset()
        xin = [self.carve(D) for _ in range(2)]
        hT = self.hT
        for tt in range(T // 128):
            s = tt % 2
            src = self.I["x"][tt * 128:(tt + 1) * 128, :] if tt < 16 else self.I["ctx"][(tt - 16) * 128:(tt - 15) * 128, :]
            self.dma("sp", xin[s], src, [], [("xin", s)])
            for g in range(2):
                pb, pk = self.bank("all")
                for kk in range(4):
                    k = g * 4 + kk
                    o = pb[:, kk * 128:(kk + 1) * 128]
                    i_ = xin[s][:, k * 128:(k + 1) * 128]
                    idn = self.ident
                    self.P.add("pe", lambda e, o=o, i_=i_, idn=idn: e.transpose(o, i_, idn),
                               reads=[("xin", s), "ident"], writes=[pk])
                dst = hT[:, g * 4:(g + 1) * 4, tt * 128:(tt + 1) * 128]
                srcp = pb.rearrange("p (k t) -> p k t", k=4)
                if g == 0:
                    self.dve(lambda e, dst=dst, srcp=srcp: e.tensor_copy(out=dst, in_=srcp), [pk], [("hw", tt, g)])
                else:
                    self.act(dst, srcp, AF.Copy, [pk], [("hw", tt, g)])
        self.P.barrier()

    def phase_mod(self, layer):
        self.reset()
        wb = [self.carve(4096, F32R).rearrange("p (k n) -> p k n", k=8) for _ in range(2)]
        aw = self.I["ada_w"][layer].rearrange("(k p) n -> p k n", p=128)
        pb, pk = self.bank("all")
        for cb in range(18):
            s = cb % 2
            self.dma("pool", wb[s], aw[:, :, cb * 512:(cb + 1) * 512], [], [("wb", s)])
            for jj in range(4):
                j = cb * 4 + jj
                for k in range(8):
                    self.mm(pb[:, 2 * j:2 * j + 2], wb[s][:, k, jj * 128:(jj + 1) * 128], self.sc[:, k, :],
                            k == 0, k == 7, [("wb", s), "sc"], [pk])
        mod, adab = self.mod, self.adab
        pm = pb[:, 0:144].rearrange("p (j c) -> p j c", c=2)
        for c in range(2):
            self.dve(lambda e, c=c: e.tensor_tensor(out=mod[:, :, c], in0=pm[:, :, c], in1=adab[:, layer, :], op=ALU.add),
                     [pk, "adab"], [("mod", c)])
        mA, mG, gains = self.mA, self.mG, self.gains
        for i in range(3):
            for c in range(2):
                sc_i = mod[:, (3 * i + 1) * 8:(3 * i + 2) * 8, c]
                g_i = mod[:, (3 * i + 2) * 8:(3 * i + 3) * 8, c]
                gn = gains[:, i * 2 + layer, :]
                self.dve(lambda e, i=i, c=c, sc_i=sc_i, gn=gn: e.scalar_tensor_tensor(
                    out=mA[:, i, :, c], in0=sc_i, scalar=1.0, in1=gn, op0=ALU.add, op1=ALU.mult),
                    [("mod", c), "gains"], [("mA", i, c)])
                fac = 1.0 if i == 1 else 0.5
                self.dve(lambda e, i=i, c=c, g_i=g_i, fac=fac: e.tensor_scalar(
                    out=mG[:, i, :, c], in0=g_i, scalar1=fac, scalar2=None, op0=ALU.mult),
                    [("mod", c)], [("mG", i, c)])
        self.P.barrier()

    def prenorm(self, i, blks, xn, xkey):
        hT, mA, mod = self.hT, self.mA, self.mod
        base = blks[0][0]
        sq = [self.carve(512, F32R) for _ in range(2)]
        tmp = [self.carve(512) for _ in range(2)]
        rt = self.carve(512)
        rr = self.carve(512)
        ones = self.ones_r
        n_sq = 0
        n_tmp = 0
        for bi, (t0, tn) in enumerate(blks):
            pb, pk = self.bank("a")
            for k in range(8):
                s = n_sq % 2
                n_sq += 1
                self.act(sq[s][:, :tn], hT[:, k, t0:t0 + tn], AF.Square, [("h", k, bi)], [("sq", s)])
                self.mm(pb[:, :tn], ones, sq[s][:, :tn], k == 0, k == 7, ["ones", ("sq", s)], [pk])
            self.act(rt[:, :tn], pb[:, :tn], AF.Sqrt, [pk], ["rt"], bias=self.eps_ap, scale=1.0 / D)
            self.dve(lambda e, tn=tn: e.reciprocal(out=rr[:, :tn], in_=rt[:, :tn]), ["rt"], ["rr"])
            for k in range(8):
                for (c, off, a, n) in seg_cols(t0, tn):
                    s = n_tmp % 2
                    n_tmp += 1
                    tm = tmp[s][:, :n]
                    self.dve(lambda e, tm=tm, k=k, a=a, n=n, c=c, off=off: e.scalar_tensor_tensor(
                        out=tm, in0=hT[:, k, a:a + n], scalar=mA[:, i, k, c:c + 1], in1=rr[:, off:off + n],
                        op0=ALU.mult, op1=ALU.mult), [("h", k, bi), "rr", ("mA", i, c)], [("tmp", s)])
                    self.act(xn[:, k, a - base:a - base + n], tm, AF.Identity, [("tmp", s), ("mod", c)],
                             [(xkey, k, bi)], bias=mod[:, 3 * i * 8 + k, c:c + 1])

    def phase_ffn(self, layer, i, t_end):
        I = self.I
        w_in = I["ffn1_w_in" if i == 0 else "ffn2_w_in"][layer].rearrange("(k p) n -> p k n", p=128)
        w_out = I["ffn1_w_out" if i == 0 else "ffn2_w_out"][layer]
        hT, mG = self.hT, self.mG
        half = HMAX
        for (h0, h1) in halves_of(t_end):
            self.reset()
            blks = blocks_of(h0, h1)
            base = blks[0][0]
            xn = self.carve(8 * half, F32R).rearrange("p (k t) -> p k t", k=8)
            actb = [self.carve(half, F32R) for _ in range(2)]
            wb = [self.carve(3072, F32R) for _ in range(3)]
            sg = [self.carve(512) for _ in range(2)]
            evb = [self.carve(512) for _ in range(4)]
            n_ev = 0
            self.prenorm(i, blks, xn, "xn")
            n_sg = 0

            def load_w(j):
                s = j % 3
                self.dma("pool", wb[s][:, 0:1024].rearrange("p (k n) -> p k n", k=8), w_in[:, :, j * 128:(j + 1) * 128],
                         [], [("wg", s)])
                self.dma("pool", wb[s][:, 1024:2048].rearrange("p (k n) -> p k n", k=8),
                         w_in[:, :, HID + j * 128:HID + (j + 1) * 128], [], [("wu", s)])
                self.dma("pool", wb[s][:, 2048:3072], w_out[j * 128:(j + 1) * 128, :], [], [("wo", s)])

            def phase_a(j, bi):
                nonlocal n_sg
                s = j % 3
                a_s = j % 2
                t0, tn = blks[bi]
                pg, pgk = self.bank("a")
                pu, puk = self.bank("a")
                lo = t0 - base
                for k in range(8):
                    self.mm(pg[:, :tn], wb[s][:, k * 128:(k + 1) * 128], xn[:, k, lo:lo + tn], k == 0, k == 7,
                            [("wg", s), ("xn", k, bi)], [pgk])
                for k in range(8):
                    self.mm(pu[:, :tn], wb[s][:, 1024 + k * 128:1024 + (k + 1) * 128], xn[:, k, lo:lo + tn],
                            k == 0, k == 7, [("wu", s), ("xn", k, bi)], [puk])
                q = n_sg % 2
                n_sg += 1
                self.act(sg[q][:, :tn], pg[:, :tn], AF.Silu, [pgk], [("sg", q)])
                dst = actb[a_s][:, lo:lo + tn]
                self.dve(lambda e, dst=dst, q=q, tn=tn, pu=pu: e.tensor_tensor(
                    out=dst, in0=sg[q][:, :tn], in1=pu[:, :tn], op=ALU.mult), [("sg", q), puk], [("act", a_s, bi)])

            def phase_b(j, bi):
                nonlocal n_ev
                s = j % 3
                a_s = j % 2
                t0, tn = blks[bi]
                lo = t0 - base
                for f in range(8):
                    py, pyk = self.bank("b")
                    self.mm(py[:, :tn], wb[s][:, 2048 + f * 128:2048 + (f + 1) * 128], actb[a_s][:, lo:lo + tn],
                            True, True, [("wo", s), ("act", a_s, bi)], [pyk])
                    for (c, off, a, n) in seg_cols(t0, tn):
                        if f % 2 == 0:
                            self.dve(lambda e, py=py, f=f, a=a, n=n, c=c, off=off: e.scalar_tensor_tensor(
                                out=hT[:, f, a:a + n], in0=py[:, off:off + n], scalar=mG[:, i, f, c:c + 1],
                                in1=hT[:, f, a:a + n], op0=ALU.mult, op1=ALU.add),
                                [pyk, ("mG", i, c), ("h", f, bi)], [("h", f, bi)])
                        else:
                            q = n_ev % 4
                            n_ev += 1
                            tq = evb[q][:, :n]
                            self.act(tq, py[:, off:off + n], AF.Copy, [pyk, ("mG", i, c)], [("evb", q)], scale=mG[:, i, f, c:c + 1])
                            self.P.add("pool", lambda e, tq=tq, f=f, a=a, n=n: e.tensor_tensor(
                                out=hT[:, f, a:a + n], in0=tq, in1=hT[:, f, a:a + n], op=ALU.add),
                                reads=[("evb", q), ("h", f, bi)], writes=[("h", f, bi)])

            load_w(0)
            load_w(1)
            for bi in range(len(blks)):
                phase_a(0, bi)
            for j in range(NJ):
                if j + 2 < NJ:
                    load_w(j + 2)
                for bi in range(len(blks)):
                    if j + 1 < NJ:
                        phase_a(j + 1, bi)
                    phase_b(j, bi)
            self.P.barrier()

    def evac(self, n, out, in_, reads, writes):
        if n % 2 == 0:
            self.act(out, in_, AF.Copy, reads, writes)
        else:
            self.dve(lambda e: e.tensor_copy(out=out, in_=in_), reads, writes)

    def phase_proj(self, layer, w_ap, fm, tok):
        w_in = w_ap.rearrange("(k p) n -> p k n", p=128)
        half = HMAX
        ntokc = sum(n for _, n, _ in tok)
        for hb, (h0, h1) in enumerate(halves_of(T)):
            self.reset()
            blks = blocks_of(h0, h1)
            base = blks[0][0]
            xn = self.carve(8 * half, F32R).rearrange("p (k t) -> p k t", k=8)
            wtok = self.carve(8 * ntokc, F32R).rearrange("p (k n) -> p k n", k=8)
            wb = [self.carve(1024, F32R).rearrange("p (k n) -> p k n", k=8) for _ in range(2)]
            stg = [self.carve(512) for _ in range(4)]
            self.prenorm(1, blks, xn, "xn")
            c0 = 0
            tokc = []
            for (col0, n, dst) in tok:
                self.dma("pool", wtok[:, :, c0:c0 + n], w_in[:, :, col0:col0 + n], [], [("wtok", c0)])
                tokc.append((c0, n, dst))
                c0 += n
            ns = 0
            xkeys = [("xn", k, bi) for k in range(8) for bi in range(len(blks))]
            for ci, (col0, n, row0) in enumerate(fm):
                s = ci % 2
                self.dma("pool", wb[s][:, :, 0:n], w_in[:, :, col0:col0 + n], [], [("wb", s)])
                for bi, (t0, tn) in enumerate(blks):
                    pb, pk = self.bank("a")
                    lo = t0 - base
                    for k in range(8):
                        self.mm(pb[:n, :tn], wb[s][:, k, 0:n], xn[:, k, lo:lo + tn], k == 0, k == 7,
                                [("wb", s), ("xn", k, bi)], [pk])
                    q = ns % 4
                    ns += 1
                    self.evac(ns, stg[q][:n, :tn], pb[:n, :tn], [pk], [("stg", q)])
                    self.dma("sp", self.zT[row0:row0 + n, t0:t0 + tn], stg[q][:n, :tn], [("stg", q)], [("zT", ci, bi, hb)])
            for tt in range((h1 - h0) // 128):
                t0 = base + tt * 128
                for (c0, n, dst) in tokc:
                    pb, pk = self.bank("b")
                    for k in range(8):
                        self.mm(pb[:, :n], xn[:, k, tt * 128:(tt + 1) * 128], wtok[:, k, c0:c0 + n], k == 0, k == 7,
                                xkeys + [("wtok", c0)], [pk])
                    q = ns % 4
                    ns += 1
                    self.evac(ns, stg[q][:, :n], pb[:, :n], [pk], [("stg", q)])
                    self.dma("sp", dst[t0:t0 + 128, :], stg[q][:, :n], [("stg", q)], [("ztok", c0, tt, hb)])
            self.P.barrier()

    def phase_fourier(self):
        I = self.I
        for (t_off, Ts, tab) in ((TL, TC, I["dft_ctx"]), (0, TL, I["dft_lat"])):
            self.reset()
            ntt = Ts // 128
            zf = self.carve(2 * Ts, F32R).rearrange("p (c t) -> p c t", c=2)
            zcs = self.carve(ntt * 512, F32R).rearrange("p (t c n) -> p t c n", t=ntt, c=2)
            G = min(4, ntt)
            tb_ = [self.carve(G * 512, F32R).rearrange("p (g n) -> p g n", g=G) for _ in range(3)]
            stg = [self.carve(512) for _ in range(2)]
            for c in range(2):
                self.dma("pool", zf[:, c, :], self.zT[c * 128:(c + 1) * 128, t_off:t_off + Ts], [], [("zf", c)])
            ne = 0
            for tt in range(ntt):
                for c in range(2):
                    pb, pk = self.bank("a")
                    self.mm(pb[:, 0:256], zf[:, c, tt * 128:(tt + 1) * 128], self.bd64, True, True,
                            [("zf", c), "bd64"], [pk])
                    ne += 1
                    self.evac(ne, zcs[:, tt, c, :], pb[:, 0:256], [pk], [("zcs", tt, c)])
            ncb = max(1, Ts // 512)
            cw = min(512, Ts)
            nl = 0
            for cb in range(ncb):
                acc = [self.bank("b") for _ in range(2)]
                first = True
                ngr = ntt // G
                for cs in range(2):
                    for g in range(ngr):
                        s = nl % 3
                        nl += 1
                        src = tab[cs].rearrange("(t p) n -> p t n", p=128)[:, g * G:(g + 1) * G, cb * cw:(cb + 1) * cw]
                        self.dma("pool", tb_[s][:, :, :cw], src, [], [("tb", s)])
                        for gi in range(G):
                            tt = g * G + gi
                            last = (cs == 1 and g == ngr - 1 and gi == G - 1)
                            for c in range(2):
                                self.mm(acc[c][0][:, :cw], zcs[:, tt, c, cs * 128:(cs + 1) * 128], tb_[s][:, gi, :cw],
                                        first, last, [("zcs", tt, c), ("tb", s)], [acc[c][1]])
                            first = False
                for c in range(2):
                    q = (cb * 2 + c) % 2
                    ne += 1
                    self.evac(ne, stg[q][:, :cw], acc[c][0][:, :cw], [acc[c][1]], [("stg", q)])
                    self.dma("sp", self.mixT[c * 128:(c + 1) * 128, t_off + cb * cw:t_off + (cb + 1) * cw], stg[q][:, :cw],
                             [("stg", q)], [("mixT", c, cb, t_off)])
            self.P.barrier()

    def phase_gla(self):
        I = self.I
        NCH = T // 64
        order = {0: [32, 33, 34, 35] + list(range(32)), 1: [35, 34, 33, 32] + list(range(31, -1, -1))}
        step_of = {d: {n: i for i, n in enumerate(order[d])} for d in (0, 1)}
        glacR, glac = self.glacR, self.glac
        SX = [glacR[0:64, 0:64], glacR[0:64, 64:128]]
        TX = [glacR[0:64, 128:192], glacR[0:64, 192:256]]
        MK = [glac[0:64, 256:320], glac[0:64, 320:384]]
        self.reset()
        zg = self.carve(T, F32R)
        self.dma("pool", zg[0:32, :], self.zT[2560:2592, :], [], ["zg"])
        self.dma("pool", zg[32:33, :], I["ones_row"], [], ["zg1"])
        qT = self.carve(T, F32R)
        kT = self.carve(T, F32R)
        vt = self.carve(NCH * 128, F32R).rearrange("p (n d) -> p n d", d=128)
        kt = self.carve(NCH * 64, F32R).rearrange("p (n d) -> p n d", d=64)
        NS = 12
        LOOK = 4
        logg = [self.carve(64, F32R) for _ in range(NS)]
        kdec = [self.carve(64, F32R) for _ in range(NS)]
        qd = [self.carve(64, F32R) for _ in range(NS)]
        kd = [self.carve(64, F32R) for _ in range(NS)]
        attm = [self.carve(64, F32R) for _ in range(NS)]
        S = [[self.carve(128, F32R) for _ in range(2)] for _ in range(2)]
        sqr = [self.carve(512, F32R) for _ in range(2)]
        ef = [self.carve(64) for _ in range(NS)]
        spf = ef
        EDf = [self.carve(64) for _ in range(NS)]
        Ebf = [self.carve(64) for _ in range(NS)]
        Enbf = [self.carve(64) for _ in range(NS)]
        oacc = self.carve(T)
        rT = self.carve(T, F32R)
        rt = self.carve(512)
        rr = self.carve(512)
        stg = [self.carve(512) for _ in range(2)]
        gn = self.vecs[:, 0:1]
        for h in range(6):
            hk = ("h", h)
            self.dma("pool", qT[0:64, :], self.zT[256 + h * 64:256 + (h + 1) * 64, :], [], ["qT"])
            self.dma("pool", kT[0:64, :], self.zT[640 + h * 64:640 + (h + 1) * 64, :], [], ["kT"])
            self.dma("pool", vt[0:64, :, :], self.vtok[:, h * 128:(h + 1) * 128].rearrange("(n p) d -> p n d", p=64), [], ["vt"])
            self.dma("pool", kt[0:64, :, :], self.ktok[:, h * 64:(h + 1) * 64].rearrange("(n p) d -> p n d", p=64), [], ["kt"])
            self.dma("pool", rT, self.zT[1792 + h * 128:1792 + (h + 1) * 128, :], [], ["rT"])
            cnt = [0]

            PS = self.ps

            def stA(st, d):
                n = order[d][st]
                sl = (st * 2 + d) % NS
                c0, c1 = n * 64, (n + 1) * 64
                pb, pk = PS[d], ("ps", d)
                self.mm(pb[0:64, 0:64], zg[0:33, c0:c1], self.gw[0:33, d * 384 + h * 64:d * 384 + (h + 1) * 64], True, True,
                        ["zg", "zg1", "gw"], [pk])
                self.act(ef[sl][0:64, :], pb[0:64, 0:64], AF.Exp, [pk], [("ef", sl)], scale=-1.0)
                self.act(ef[sl][0:64, :], ef[sl][0:64, :], AF.Ln, [("ef", sl)], [("ef", sl)], bias=self.one_ap[0:64, :])
                self.dve(lambda e: e.tensor_scalar(out=logg[sl][0:64, :], in0=ef[sl][0:64, :], scalar1=-1.0 / 16.0,
                                                   scalar2=None, op0=ALU.mult), [("ef", sl)], [("logg", sl)])

            def stB(st, d):
                n = order[d][st]
                sl = (st * 2 + d) % NS
                c0, c1 = n * 64, (n + 1) * 64
                pb, pk = PS[2 + d], ("ps", 2 + d)
                pD = pb[0:64, 0:64]
                pB = pb[0:64, 64:128]
                self.mm(pD, SX[d], logg[sl][0:64, :], True, True, ["glacR", ("logg", sl)], [pk])
                self.mm(pB, logg[sl][0:64, :], TX[d], True, True, ["glacR", ("logg", sl)], [pk])
                self.act(EDf[sl][0:64, :], pD, AF.Exp, [pk], [("ED", sl)])
                self.act(Ebf[sl][0:64, :], pB, AF.Exp, [pk], [("Eb", sl)])
                self.act(Enbf[sl][0:64, :], pB, AF.Exp, [pk], [("Enb", sl)], scale=-1.0)
                self.dve(lambda e: e.tensor_tensor(out=kdec[sl][0:64, :], in0=EDf[sl][0:64, :], in1=kt[0:64, n, :], op=ALU.mult),
                         [("ED", sl), "kt"], [("kdec", sl)])
                self.dve(lambda e: e.scalar_tensor_tensor(out=qd[sl][0:64, :], in0=qT[0:64, c0:c1], scalar=0.125,
                                                          in1=Ebf[sl][0:64, :], op0=ALU.mult, op1=ALU.mult),
                         ["qT", ("Eb", sl)], [("qd", sl)])
                self.dve(lambda e: e.tensor_tensor(out=kd[sl][0:64, :], in0=kT[0:64, c0:c1], in1=Enbf[sl][0:64, :], op=ALU.mult),
                         ["kT", ("Enb", sl)], [("kd", sl)])

            def stC(st, d):
                sl = (st * 2 + d) % NS
                pA, pAk = PS[4 + d], ("ps", 4 + d)
                self.mm(pA[0:64, 0:64], kd[sl][0:64, :], qd[sl][0:64, :], True, True, [("kd", sl), ("qd", sl)], [pAk])
                self.dve(lambda e: e.tensor_tensor(out=attm[sl][0:64, :], in0=pA[0:64, 0:64], in1=MK[d], op=ALU.mult),
                         [pAk, "glac"], [("attm", sl)])

            def stR(st, d):
                n = order[d][st]
                sl = (st * 2 + d) % NS
                c0, c1 = n * 64, (n + 1) * 64
                cur, nxt = S[d][st % 2], S[d][(st + 1) % 2]
                pb, pk = PS[6 + d], ("ps", 6 + d)
                pO = pb[:, 0:64]
                pS = pb[0:64, 128:256]
                self.mm(pO, vt[0:64, n, :], attm[sl][0:64, :], True, st == 0, ["vt", ("attm", sl)], [pk])
                if st > 0:
                    self.mm(pO, cur[0:64, :], qd[sl][0:64, :], False, True, [("S", d, st % 2), ("qd", sl)], [pk])
                self.mm(pS, kdec[sl][0:64, :], vt[0:64, n, :], True, True, [("kdec", sl), "vt"], [pk])
                if st == 0:
                    self.dve(lambda e: e.tensor_copy(out=nxt[0:64, :], in_=pS), [pk], [("S", d, (st + 1) % 2)])
                else:
                    col = 63 if d == 0 else 0
                    self.dve(lambda e: e.scalar_tensor_tensor(out=nxt[0:64, :], in0=cur[0:64, :], scalar=Ebf[sl][0:64, col:col + 1],
                                                              in1=pS, op0=ALU.mult, op1=ALU.add),
                             [("S", d, st % 2), ("Eb", sl), pk], [("S", d, (st + 1) % 2)])
                other = step_of[1 - d][n]
                first_touch = (st, d) < (other, 1 - d)
                if first_touch:
                    self.act(oacc[:, c0:c1], pO, AF.Copy, [pk], [("oacc", n)])
                else:
                    self.dve(lambda e: e.tensor_tensor(out=oacc[:, c0:c1], in0=pO, in1=oacc[:, c0:c1], op=ALU.add),
                             [pk, ("oacc", n)], [("oacc", n)])

            for t in range(-3, NCH):
                for (fn, off) in ((stA, 3), (stB, 2), (stC, 1), (stR, 0)):
                    st = t + off
                    if 0 <= st < NCH:
                        fn(st, 0)
                        fn(st, 1)
            self.act(rT, rT, AF.Silu, ["rT"], ["rT"])
            for bi, (t0, tn) in enumerate(blocks_of(0, T, 512)):
                s = bi % 2
                okeys = [("oacc", n) for n in range(t0 // 64, (t0 + tn) // 64)]
                self.act(sqr[s][:, :tn], oacc[:, t0:t0 + tn], AF.Square, okeys, [("sqr", s)])
                pb, pk = self.bank("a")
                self.mm(pb[:, :tn], self.ones_r, sqr[s][:, :tn], True, True, ["ones", ("sqr", s)], [pk])
                self.act(rt[:, :tn], pb[:, :tn], AF.Sqrt, [pk], ["rt"], bias=self.eps_ap, scale=1.0 / 128.0)
                self.dve(lambda e, tn=tn: e.reciprocal(out=rr[:, :tn], in_=rt[:, :tn]), ["rt"], ["rr"])
                self.dve(lambda e, t0=t0, tn=tn, s=s: e.scalar_tensor_tensor(out=stg[s][:, :tn], in0=oacc[:, t0:t0 + tn], scalar=gn,
                                                                        in1=rr[:, :tn], op0=ALU.mult, op1=ALU.mult),
                         okeys + ["rr", "vecs"], [("stg", s)])
                self.dve(lambda e, t0=t0, tn=tn, s=s: e.tensor_tensor(out=stg[s][:, :tn], in0=stg[s][:, :tn], in1=rT[:, t0:t0 + tn],
                                                                 op=ALU.mult), [("stg", s), "rT"], [("stg", s)])
                self.dma("sp", self.mixT[256 + h * 128:256 + (h + 1) * 128, t0:t0 + tn], stg[s][:, :tn], [("stg", s)],
                         [("mixT", h, bi)])
        self.P.barrier()

    def phase_outproj(self, layer, t_end):
        self.reset()
        hT, mG = self.hT, self.mG
        w = self.carve(8 * D, F32R).rearrange("p (k n) -> p k n", k=8)
        self.dma("pool", w, self.I["mix_w_out"][layer].rearrange("(k p) n -> p k n", p=128), [], ["wmo"])
        mb = [self.carve(8 * 512, F32R).rearrange("p (k t) -> p k t", k=8) for _ in range(2)]
        for bi, (t0, tn) in enumerate(blocks_of(0, t_end)):
            s = bi % 2
            self.dma("pool", mb[s][:, :, :tn], self.mixT.rearrange("(k p) t -> p k t", p=128)[:, :, t0:t0 + tn], [], [("mb", s)])
            for f in range(8):
                pb, pk = self.bank("all")
                for k in range(8):
                    self.mm(pb[:, :tn], w[:, k, f * 128:(f + 1) * 128], mb[s][:, k, :tn], k == 0, k == 7, ["wmo", ("mb", s)], [pk])
                for (c, off, a, n) in seg_cols(t0, tn):
                    self.dve(lambda e, pb=pb, f=f, a=a, n=n, c=c, off=off: e.scalar_tensor_tensor(
                        out=hT[:, f, a:a + n], in0=pb[:, off:off + n], scalar=mG[:, 1, f, c:c + 1],
                        in1=hT[:, f, a:a + n], op0=ALU.mult, op1=ALU.add),
                        [pk, ("mG", 1, c), ("h", f, bi)], [("h", f, bi)])
        self.P.barrier()

    def phase_conv(self):
        self.reset()
        cv = self.convv
        bufs = [self.carve(TL) for _ in range(3)]
        for c in range(2):
            zb, zc, zx = bufs
            for j, b in enumerate(bufs):
                self.dma("sp", b, self.zT[j * 256 + c * 128:j * 256 + (c + 1) * 128, 0:TL], [], [("cb", j)])
            self.dve(lambda e: e.tensor_tensor(out=zc, in0=zc, in1=zx, op=ALU.mult), [("cb", 1), ("cb", 2)], [("cb", 1)])
            self.dve(lambda e, c=c: e.tensor_scalar(out=zx, in0=zc, scalar1=cv[:, c * 4 + 1:c * 4 + 2], scalar2=cv[:, c * 4 + 3:c * 4 + 4],
                                               op0=ALU.mult, op1=ALU.add), [("cb", 1), "convv"], [("cb", 2)])
            self.dve(lambda e, c=c: e.scalar_tensor_tensor(out=zx[:, 1:TL], in0=zc[:, 0:TL - 1], scalar=cv[:, c * 4:c * 4 + 1],
                                                      in1=zx[:, 1:TL], op0=ALU.mult, op1=ALU.add), [("cb", 1), ("cb", 2)], [("cb", 2)])
            self.dve(lambda e, c=c: e.scalar_tensor_tensor(out=zx[:, 0:TL - 1], in0=zc[:, 1:TL], scalar=cv[:, c * 4 + 2:c * 4 + 3],
                                                      in1=zx[:, 0:TL - 1], op0=ALU.mult, op1=ALU.add), [("cb", 1), ("cb", 2)], [("cb", 2)])
            self.dve(lambda e: e.tensor_tensor(out=zb, in0=zb, in1=zx, op=ALU.mult), [("cb", 0), ("cb", 2)], [("cb", 0)])
            self.dma("sp", self.mixT[c * 128:(c + 1) * 128, 0:TL], zb, [("cb", 0)], [("mixc", c)])
        self.P.barrier()

    def phase_dattn(self):
        I = self.I
        self.reset()
        cosT = self.carve(TL)
        sinT = self.carve(TL)
        fb = [self.carve(512) for _ in range(6)]
        qT = self.carve(TL, F32R)
        qz = self.carve(2 * TL, F32R).rearrange("p (s t) -> p s t", s=2)
        kT = self.carve(T, F32R)
        vt = self.carve(T, F32R).rearrange("p (n d) -> p n d", d=128)
        NE = 4
        E = [self.carve(512, F32R) for _ in range(NE)]
        sqr = self.carve(512, F32R)
        self.dma("sp", cosT, I["rope"][0], [], ["cosT"])
        self.dma("sp", sinT, I["rope"][1], [], ["sinT"])
        self.pools["s4"] = [0, 1, 2, 3]
        ne = 0
        self.dve(lambda e: e.tensor_scalar(out=qz[64:128, 0, :], in0=cosT[64:128, :], scalar1=0.0, scalar2=None, op0=ALU.mult),
                 ["cosT"], ["qz0"])
        self.dve(lambda e: e.tensor_scalar(out=qz[0:64, 1, :], in0=cosT[0:64, :], scalar1=0.0, scalar2=None, op0=ALU.mult),
                 ["cosT"], ["qz1"])
        for h in range(6):
            self.dma("pool", qT, self.zT[768 + h * 128:768 + (h + 1) * 128, 0:TL], [], ["qT"])
            self.dma("pool", kT, self.zT[1536 + h * 128:1536 + (h + 1) * 128, :], [], ["kT"])
            self.dma("pool", vt, self.vtok[:, h * 128:(h + 1) * 128].rearrange("(n p) d -> p n d", p=128), [], ["vt"])
            for (x, xk) in ((qT, "qT"), (kT, "kT")):
                for qb in range(4):
                    c0, c1 = qb * 512, (qb + 1) * 512
                    pb, pk = self.bank("s4")
                    self.mm(pb[:, :], self.permT, x[:, c0:c1], True, True, ["permT", xk], [pk])
                    t1, t2 = fb[(qb % 2) * 2], fb[(qb % 2) * 2 + 1]
                    k1, k2 = ("fb", (qb % 2) * 2), ("fb", (qb % 2) * 2 + 1)
                    self.dve(lambda e, t1=t1, pb=pb, c0=c0, c1=c1: e.tensor_tensor(out=t1, in0=pb[:, :], in1=sinT[:, c0:c1], op=ALU.mult),
                             [pk, "sinT"], [k1])
                    self.dve(lambda e, t2=t2, x=x, c0=c0, c1=c1: e.tensor_tensor(out=t2, in0=x[:, c0:c1], in1=cosT[:, c0:c1], op=ALU.mult),
                             [xk, "cosT"], [k2])
                    if xk == "kT":
                        self.dve(lambda e, t1=t1, t2=t2, x=x, c0=c0, c1=c1: e.tensor_tensor(out=x[:, c0:c1], in0=t1, in1=t2, op=ALU.add),
                                 [k1, k2], [xk])
                    else:
                        self.dve(lambda e, t1=t1, t2=t2, c0=c0, c1=c1: e.tensor_tensor(out=qz[0:64, 0, c0:c1], in0=t1[0:64, :],
                                                                                   in1=t2[0:64, :], op=ALU.add), [k1, k2], ["qzr0"])
                        self.dve(lambda e, t1=t1, t2=t2, c0=c0, c1=c1: e.tensor_tensor(out=qz[64:128, 1, c0:c1], in0=t1[64:128, :],
                                                                                   in1=t2[64:128, :], op=ALU.add), [k1, k2], ["qzr1"])
            for qb in range(4):
                c0, c1 = qb * 512, (qb + 1) * 512
                acc = {}
                for sub in range(2):
                    acc[("Z", sub)] = (self.ps[4 + sub * 2], ("ps", 4 + sub * 2))
                    acc[("O", sub)] = (self.ps[5 + sub * 2], ("ps", 5 + sub * 2))
                its = [(kt_, sub) for kt_ in range(T // 128) for sub in range(2)]
                slots = {}

                def score(ii):
                    nonlocal ne
                    kt_, sub = its[ii]
                    pS, pSk = self.bank("s4")
                    self.mm(pS[:, :], kT[:, kt_ * 128:(kt_ + 1) * 128], qz[:, sub, c0:c1], True, True,
                            ["kT", "qz0", "qz1", "qzr0", "qzr1"], [pSk])
                    es = ne % NE
                    ne += 1
                    self.act(E[es], pS[:, :], AF.Exp, [pSk], [("E", es)], scale=0.125)
                    slots[ii] = es

                score(0)
                score(1)
                for ii, (kt_, sub) in enumerate(its):
                    if ii + 2 < len(its):
                        score(ii + 2)
                    es = slots[ii]
                    zb_, zk = acc[("Z", sub)]
                    ob_, ok = acc[("O", sub)]
                    self.mm(zb_[:, :], self.ones_r, E[es], kt_ == 0, kt_ == T // 128 - 1, ["ones", ("E", es)], [zk])
                    self.mm(ob_[:, :], vt[:, kt_, :], E[es], kt_ == 0, kt_ == T // 128 - 1, ["vt", ("E", es)], [ok])
                b0, b1, b2, b3 = fb[0], fb[1], fb[2], fb[3]
                bo = fb[4 + qb % 2]
                kk = [("fb", i) for i in range(4)]
                ko = ("fb", 4 + qb % 2)
                Z0, O0, Z1, O1 = acc[("Z", 0)], acc[("O", 0)], acc[("Z", 1)], acc[("O", 1)]
                self.act(b0, Z0[0][:, :], AF.Copy, [Z0[1]], [kk[0]])
                self.act(b1, Z1[0][:, :], AF.Copy, [Z1[1]], [kk[1]])
                self.dve(lambda e, O0=O0: e.tensor_copy(out=b2, in_=O0[0][:, :]), [O0[1]], [kk[2]])
                self.dve(lambda e, O1=O1: e.tensor_copy(out=b3, in_=O1[0][:, :]), [O1[1]], [kk[3]])
                self.dve(lambda e: e.reciprocal(out=b0, in_=b0), [kk[0]], [kk[0]])
                self.dve(lambda e: e.reciprocal(out=b1, in_=b1), [kk[1]], [kk[1]])
                self.dve(lambda e: e.tensor_tensor(out=b2, in0=b2, in1=b0, op=ALU.mult), [kk[2], kk[0]], [kk[2]])
                self.dve(lambda e: e.tensor_tensor(out=b3, in0=b3, in1=b1, op=ALU.mult), [kk[3], kk[1]], [kk[3]])
                neglam = self.neglam
                self.dve(lambda e: e.scalar_tensor_tensor(out=b2, in0=b3, scalar=neglam, in1=b2, op0=ALU.mult, op1=ALU.add),
                         [kk[3], kk[2], "neglam"], [kk[2]])
                self.act(sqr, b2, AF.Square, [kk[2]], ["sqr"])
                pb, pk = self.bank("s4")
                self.mm(pb[:, :], self.ones_r, sqr, True, True, ["ones", "sqr"], [pk])
                self.act(b0, pb[:, :], AF.Sqrt, [pk], [kk[0]], bias=self.eps_ap, scale=1.0 / 128.0)
                self.dve(lambda e: e.reciprocal(out=b1, in_=b0), [kk[0]], [kk[1]])
                dnl = self.vecs[:, 2:3]
                self.dve(lambda e, bo=bo: e.scalar_tensor_tensor(out=bo, in0=b2, scalar=dnl, in1=b1, op0=ALU.mult, op1=ALU.mult),
                         [kk[2], kk[1], "vecs2"], [ko])
                b3 = bo
                kk = kk[:3] + [ko]
                self.dma("sp", self.mixT[256 + h * 128:256 + (h + 1) * 128, c0:c1], b3, [kk[3]], [("mixo", h, qb)])
        self.P.barrier()

    def phase_final(self):
        self.reset()
        hT = self.hT
        gbc = self.carve(D)
        self.dma("sp", gbc, self.I["fnorm_bc"], [], ["gbc"])
        xt = [self.carve(D) for _ in range(2)]
        yt = [self.carve(D) for _ in range(2)]
        junk = self.carve(D)
        ss = [self.carve(1) for _ in range(2)]
        rs = [self.carve(1) for _ in range(2)]
        outs = []
        for tt in range(TL // 128):
            s = tt % 2
            for g in range(2):
                pb, pk = self.bank("all")
                for kk in range(4):
                    k = g * 4 + kk
                    o = pb[:, kk * 128:(kk + 1) * 128]
                    i_ = hT[:, k, tt * 128:(tt + 1) * 128]
                    idn = self.ident
                    self.P.add("pe", lambda e, o=o, i_=i_, idn=idn: e.transpose(o, i_, idn), reads=["ident"], writes=[pk])
                self.evac(g, xt[s][:, g * 512:(g + 1) * 512], pb[:, :], [pk], [("xt", s, g)])
            xs, ys, sss, rss = xt[s], yt[s], ss[s], rs[s]
            self.P.add("act", lambda e, xs=xs, sss=sss: e.activation(out=junk, in_=xs, func=AF.Square, accum_out=sss),
                       reads=[("xt", s, 0), ("xt", s, 1)], writes=[("ss", s), "junk"])
            self.act(rss, sss, AF.Sqrt, [("ss", s)], [("rs0", s)], bias=self.eps_ap, scale=1.0 / D)
            self.dve(lambda e, rss=rss: e.reciprocal(out=rss, in_=rss), [("rs0", s)], [("rs", s)])
            self.dve(lambda e, xs=xs, ys=ys, rss=rss: e.scalar_tensor_tensor(out=ys, in0=xs, scalar=rss, in1=gbc, op0=ALU.mult,
                                                                         op1=ALU.mult),
                     [("xt", s, 0), ("xt", s, 1), ("rs", s), "gbc"], [("yt", s)])
            self.dma("sp", self.out[tt * 128:(tt + 1) * 128, :], ys, [("yt", s)], [("out", tt)])
            outs.append(("out", tt))
        self.P.add("sp", None, reads=outs)

    def phase_dump(self):
        self.dma("sp", self.dbg, self.hT.rearrange("p k t -> p (k t)"), [], ["dbg"])
        self.P.add("sp", None, reads=["dbg"])


def fm(v):
    v = np.asarray(v, np.float32)
    lead = v.shape[:-1]
    r = v.reshape(lead + (v.shape[-1] // 128, 128))
    return np.ascontiguousarray(np.moveaxis(r, -1, 0))


def prep_inputs(inp, b):
    m = {}
    m["x"] = np.ascontiguousarray(inp["x"][b])
    m["ctx"] = np.ascontiguousarray(inp["ctx"][b])
    cond = np.stack([fm(inp["c"][b]), fm(inp["c_ctx"])], axis=-1)
    m["cond"] = np.ascontiguousarray(cond.reshape(128, 16))
    m["ada_w"] = inp["ada_w"]
    m["ada_b"] = np.ascontiguousarray(fm(inp["ada_b"]).reshape(128, 144))
    g = np.stack([inp["norm_ffn1"][0], inp["norm_ffn1"][1], inp["norm_mix"][0], inp["norm_mix"][1],
                  inp["norm_ffn2"][0], inp["norm_ffn2"][1], inp["final_norm"]], axis=0)
    m["gains"] = np.ascontiguousarray(fm(g).reshape(128, 56))
    for k in ("ffn1_w_in", "ffn1_w_out", "ffn2_w_in", "ffn2_w_out"):
        m[k] = inp[k]
    m["ident"] = np.eye(128, dtype=np.float32)
    m["mix_w_out"] = inp["mix_w_out"]
    m["even_w_in"] = np.ascontiguousarray(inp["even_w_in"][0])
    m.update(CONSTS())
    gw = np.zeros((33, 768), np.float32)
    gw[0:16, 0:384] = inp["gla_gate_w"][0, 0]
    gw[32, 0:384] = inp["gla_gate_b"][0, 0]
    gw[16:32, 384:768] = inp["gla_gate_w"][0, 1]
    gw[32, 384:768] = inp["gla_gate_b"][0, 1]
    m["gw_pad"] = gw
    v = np.zeros((128, 8), np.float32)
    v[:, 0] = inp["gla_norm"][0]
    v[:, 1] = inp["diff_norm"][0]
    m["vecs128"] = v
    m["odd_w_in"] = np.ascontiguousarray(inp["odd_w_in"][0])
    m["lamv"] = np.ascontiguousarray(np.stack([inp["lambda_q1"][0], inp["lambda_k1"][0], inp["lambda_q2"][0],
                                               inp["lambda_k2"][0]], axis=1).astype(np.float32))
    cv = np.zeros((128, 8), np.float32)
    for c in range(2):
        for j in range(3):
            cv[:, c * 4 + j] = inp["conv_w"][0, j, c * 128:(c + 1) * 128]
        cv[:, c * 4 + 3] = inp["conv_b"][0, c * 128:(c + 1) * 128]
    m["convv"] = cv
    m["fnorm_bc"] = np.ascontiguousarray(np.broadcast_to(inp["final_norm"].astype(np.float32)[None, :], (128, D)))
    return m


_CONSTS = None


def CONSTS():
    global _CONSTS
    if _CONSTS is not None:
        return _CONSTS
    c = {}

    def dft(n, scale):
        i = np.arange(n, dtype=np.int64)
        ang = 2.0 * np.pi * ((i[:, None] * i[None, :]) % n).astype(np.float64) / n
        return (np.cos(ang) * scale).astype(np.float32), (-np.sin(ang) * scale).astype(np.float32)

    cl, sl = dft(TL, 1.0 / math.sqrt(TL * 64.0))
    c["dft_lat"] = np.stack([cl, sl])
    cc, sc_ = dft(TC, 1.0 / math.sqrt(TC * 64.0))
    c["dft_ctx"] = np.stack([cc, sc_])
    i = np.arange(64, dtype=np.int64)
    a64 = 2.0 * np.pi * ((i[:, None] * i[None, :]) % 64).astype(np.float64) / 64
    bd = np.zeros((128, 256), np.float32)
    for g in range(2):
        bd[g * 64:(g + 1) * 64, g * 64:(g + 1) * 64] = np.cos(a64)
        bd[g * 64:(g + 1) * 64, 128 + g * 64:128 + (g + 1) * 64] = np.sin(a64)
    c["bd64"] = bd
    tp, t = i[:, None], i[None, :]
    g = np.zeros((64, 6 * 64), np.float32)
    g[:, 0:64] = (tp > t)
    g[:, 64:128] = (tp < t)
    g[:, 128:192] = (tp <= t)
    g[:, 192:256] = (tp >= t)
    g[:, 256:320] = (t >= tp)
    g[:, 320:384] = (t <= tp)
    c["gla_c"] = g
    c["ones_row"] = np.ones((1, T), np.float32)
    inv = (np.float32(10000.0) ** (-(np.arange(16, dtype=np.float32)) / np.float32(16))).astype(np.float32)
    tpos = np.arange(TL)
    row = (tpos // 64).astype(np.float32)
    colp = (tpos % 64).astype(np.float32)
    rope = np.zeros((2, 128, TL), np.float32)
    perm = np.zeros((128, 128), np.float32)
    for p in range(128):
        d = p % 64
        axis, half, fi = d // 32, (d % 32) // 16, d % 16
        ang = ((row if axis == 0 else colp) * inv[fi]).astype(np.float32).astype(np.float64)
        rope[0, p] = np.cos(ang)
        rope[1, p] = (-np.sin(ang)) if half == 0 else np.sin(ang)
        partner = p + 16 if half == 0 else p - 16
        perm[partner, p] = 1.0
    c["rope"] = rope
    c["permT"] = perm
    _CONSTS = c
    return c


def kernel(**inp):
    inp = {k: np.asarray(v) for k, v in inp.items()}
    bld = Builder()
    nc = bld.build()
    in_maps = [prep_inputs(inp, b) for b in range(8)]
    res = run_bass_kernel_spmd(nc, in_maps, core_ids=list(range(8)))
    return np.stack([r["out"] for r in res.results], axis=0)
```
